# Optimizing a Trainium2 kernel written in Bass

```python
import jax, jax.numpy as jnp
from jax import lax
import numpy as np

D_MODEL = 1024
BATCH = 2
SEQ = 8192
DEPTH = 1

POOL_WINDOWS = (2, 4, 8, 16)
N_POOL_GROUPS = len(POOL_WINDOWS)
D_POOL = D_MODEL
POOL_GW = D_POOL // N_POOL_GROUPS
POOL_OUT_GW = D_MODEL // N_POOL_GROUPS
D_CONV = D_MODEL
CONV_K = 3
D_IN = D_POOL + 3 * D_CONV + 2 * D_MODEL
D_FF = 2816
FFN_K = 3
N_MOD = 6
EPS = 1e-6

kernel_name = "hybrid_pool_shortconv_convffn_block"


def rmsnorm(x, g):
    xf = x.astype(jnp.float32)
    y = xf * lax.rsqrt(jnp.mean(xf * xf, axis=-1, keepdims=True) + EPS)
    return (y * g.astype(jnp.float32)).astype(x.dtype)


def causal_dwconv(x, w, b):
    k = w.shape[0]
    s = x.shape[1]
    xp = jnp.pad(x, ((0, 0), (k - 1, 0), (0, 0)))
    y = b
    for i in range(k):
        y = y + w[i] * xp[:, i:i + s]
    return y


def causal_multiscale_pool(u):
    bsz, s, _ = u.shape
    ug = u.reshape(bsz, s, N_POOL_GROUPS, POOL_GW)
    cs = jnp.cumsum(ug.astype(jnp.float32), axis=1)
    cs0 = jnp.pad(cs, ((0, 0), (1, 0), (0, 0), (0, 0)))
    t1 = jnp.arange(1, s + 1, dtype=jnp.float32)
    outs = []
    for g, w in enumerate(POOL_WINDOWS):
        upper = cs0[:, 1:, g]
        lower = jnp.pad(cs0[:, :s + 1 - w, g], ((0, 0), (w - 1, 0), (0, 0)))
        cnt = jnp.minimum(t1, float(w))[None, :, None]
        outs.append((upper - lower) / cnt)
    pooled = jnp.stack(outs, axis=2).astype(u.dtype)
    return pooled - ug


def setup_inputs(seed: int = 0) -> dict:
    key = jax.random.key(seed)
    ks = jax.random.split(key, 20)
    L, D = DEPTH, D_MODEL
    nrm = lambda k, shp, fan: jax.random.normal(k, shp, jnp.float32) * (fan ** -0.5)
    gain = lambda k, n: 1.0 + 0.05 * jax.random.normal(k, (L, n), jnp.float32)
    return {
        "x": jax.random.normal(ks[0], (BATCH, SEQ, D), jnp.float32),
        "c": jax.random.normal(ks[1], (BATCH, D), jnp.float32),
        "g_pre_mix": gain(ks[2], D),
        "g_post_mix": gain(ks[3], D),
        "g_pre_ffn": gain(ks[4], D),
        "g_post_ffn": gain(ks[5], D),
        "w_ada": 0.5 * nrm(ks[6], (L, D, N_MOD * D), D),
        "b_ada": 0.01 * jax.random.normal(ks[7], (L, N_MOD * D), jnp.float32),
        "w_in": nrm(ks[8], (L, D, D_IN), D),
        "w_pool": nrm(ks[9], (L, N_POOL_GROUPS, POOL_GW, POOL_OUT_GW), POOL_GW),
        "pool_scale": gain(ks[10], D),
        "conv_w": nrm(ks[11], (L, CONV_K, D_CONV), CONV_K),
        "conv_b": 0.01 * jax.random.normal(ks[12], (L, D_CONV), jnp.float32),
        "w_bout": nrm(ks[13], (L, D_CONV, D), D_CONV),
        "w_o": nrm(ks[14], (L, D, D), D),
        "w_up": nrm(ks[15], (L, D, 2 * D_FF), D),
        "ffn_conv_w": nrm(ks[16], (L, FFN_K, 2 * D_FF), FFN_K),
        "ffn_conv_b": 0.01 * jax.random.normal(ks[17], (L, 2 * D_FF), jnp.float32),
        "w_down": nrm(ks[18], (L, D_FF, D), D_FF),
    }


def reference(x, c, g_pre_mix, g_post_mix, g_pre_ffn, g_post_ffn, w_ada, b_ada, w_in, w_pool,
              pool_scale, conv_w, conv_b, w_bout, w_o, w_up, ffn_conv_w, ffn_conv_b, w_down):
    bsz, s, d = x.shape
    for l in range(DEPTH):
        mod = c @ w_ada[l] + b_ada[l]
        sh1, sc1, gt1, sh2, sc2, gt2 = [m[:, None, :] for m in jnp.split(mod, N_MOD, axis=-1)]

        h = rmsnorm(x, g_pre_mix[l]) * (1.0 + sc1) + sh1
        proj = h @ w_in[l]
        u_pool, u_x, u_b, u_c, z_a, z_b = jnp.split(
            proj, np.cumsum([D_POOL, D_CONV, D_CONV, D_CONV, D_MODEL])[:].tolist(), axis=-1)

        pg = causal_multiscale_pool(u_pool)
        y_a = jnp.einsum('bsgc,gcd->bsgd', pg, w_pool[l]).reshape(bsz, s, d) * pool_scale[l]

        y_b = (u_b * causal_dwconv(u_c * u_x, conv_w[l], conv_b[l])) @ w_bout[l]

        merged = jax.nn.sigmoid(z_a) * y_a + jax.nn.sigmoid(z_b) * y_b
        x = x + gt1 * rmsnorm(merged @ w_o[l], g_post_mix[l])

        h = rmsnorm(x, g_pre_ffn[l]) * (1.0 + sc2) + sh2
        up = causal_dwconv(h @ w_up[l], ffn_conv_w[l], ffn_conv_b[l])
        gate, val = jnp.split(up, 2, axis=-1)
        ff = (jax.nn.gelu(gate, approximate=True) * val) @ w_down[l]
        x = x + gt2 * rmsnorm(ff, g_post_ffn[l])
    return x
```

```python
import numpy as np
import concourse.bass as bass
import concourse.mybir as mybir
from concourse.bass_utils import run_bass_kernel_spmd

F32 = mybir.dt.float32
BF16 = mybir.dt.bfloat16
AF = mybir.ActivationFunctionType
ALU = mybir.AluOpType

D = 1024
DFF = 2816
NCH = 8
NFF = 22
NG = 2
GT = 1024
HALO = 20
NT = GT + HALO
TS = 348
NTILE = NT // TS
SPC = 48
EPS = 1e-6
NSLOT = 6
SLOT_ELEMS = 4096
N_CORES = 8
USE_SCHED = True
TILE_CHAIN = True
TILE_IO = True
TAILC = 4
XSLOTS = 4
BATCH_PRE = True
SQ_DVE_POST = 4
SQ_DVE_PRE = 8
CORE_TOK = 2048

C_GPRE1, C_GPOST1, C_GPRE2, C_GPOST2 = 0, 8, 16, 24
C_BADA = 32
C_PSC = 80
C_CW = 88
C_CB = 112
C_FW = 120
C_FB = 252
NPRM = 296


class V:
    __slots__ = ("ap", "iv")

    def __init__(self, ap, iv):
        self.ap = ap
        self.iv = iv


class Buf:
    def __init__(self, t, arena, off_b, dtype, ncols):
        self.t, self.arena, self.off, self.dtype, self.ncols = t, arena, off_b, dtype, ncols
        self.es = 4 if dtype == F32 else 2
        assert off_b % 4 == 0

    def v(self, lo=0, hi=None):
        if hi is None:
            hi = self.ncols
        assert 0 <= lo < hi <= self.ncols, (lo, hi, self.ncols)
        b0 = self.off + lo * self.es
        b1 = self.off + hi * self.es
        assert b0 % 2 == 0
        a = self.t[:, b0 // 2: b1 // 2]
        if self.dtype == F32:
            assert b0 % 4 == 0
            a = a.bitcast(F32)
        return V(a, (self.arena, b0, b1))


class Prog:
    COMPUTE = ("pe", "act", "dve", "pool")

    def __init__(self):
        self.ops = []
        self.recs = {}

    def add(self, eng, fn, reads, writes, key=None, ndma=0, dur=0.5, xfer=0.0):
        oid = len(self.ops)
        deps = {}
        isdma = key is not None
        for (a, lo, hi) in reads:
            for r in self.recs.setdefault(a, []):
                if r[3] and r[0] < hi and lo < r[1]:
                    deps[r[2]] = True
        for (a, lo, hi) in writes:
            for r in self.recs.setdefault(a, []):
                if r[0] < hi and lo < r[1]:
                    deps.setdefault(r[2], False)
        for (a, lo, hi) in writes:
            L = self.recs[a]
            L[:] = [r for r in L if not (lo <= r[0] and r[1] <= hi)]
            L.append([lo, hi, oid, True, eng, isdma])
        for (a, lo, hi) in reads:
            L = self.recs[a]
            L.append([lo, hi, oid, False, eng, isdma])
        deps.pop(oid, None)
        self.ops.append(dict(eng=eng, fn=fn, deps=deps, key=key, ndma=ndma, dur=dur, xfer=xfer))
        return oid

    def kept_deps(self):
        ops = self.ops
        n = len(ops)
        kept = [None] * n
        for i, op in enumerate(ops):
            kd = []
            for d, raw in op["deps"].items():
                dop = ops[d]
                d_dma = dop["key"] is not None
                o_dma = op["key"] is not None
                if not d_dma and not o_dma and dop["eng"] == op["eng"]:
                    if op["eng"] == "pe":
                        continue
                kd.append(d)
            kept[i] = kd
        return kept

    def schedule(self):
        ops = self.ops
        n = len(ops)
        LAT = 0.7
        succ = [[] for _ in range(n)]
        indeg = [0] * n
        last_sp = None
        for i, op in enumerate(ops):
            ds = set(op["deps"].keys())
            if op["eng"] == "sp":
                if last_sp is not None:
                    ds.add(last_sp)
                last_sp = i
            for d in ds:
                succ[d].append(i)
                indeg[i] += 1
        blev = [0.0] * n
        for i in range(n - 1, -1, -1):
            m = 0.0
            for j in succ[i]:
                if blev[j] > m:
                    m = blev[j]
            blev[i] = m + ops[i]["dur"] + ops[i]["xfer"] + LAT
        t_eng = {e: 7.0 for e in ("pe", "act", "dve", "pool", "sp")}
        pipe = [0.0]
        DMA_FIXED = 2.0
        ready = {e: [] for e in t_eng}
        rtime = [0.0] * n
        done = [0.0] * n
        for i in range(n):
            if indeg[i] == 0:
                ready[ops[i]["eng"]].append(i)
        order = []
        while len(order) < n:
            best = None
            for e, L in ready.items():
                if not L:
                    continue
                te = t_eng[e]
                bi = None
                bkey = None
                for i in L:
                    st = rtime[i] if rtime[i] > te else te
                    key = (st, -blev[i], i)
                    if bkey is None or key < bkey:
                        bkey = key
                        bi = i
                if best is None or bkey < best[0]:
                    best = (bkey, e, bi)
            (st, _, _), e, i = best
            ready[e].remove(i)
            op = ops[i]
            t_eng[e] = st + op["dur"]
            if op["key"] is not None:
                beg = max(st + op["dur"], pipe[0])
                pipe[0] = beg + op["xfer"] - DMA_FIXED
                done[i] = max(beg + op["xfer"] - DMA_FIXED, st + op["dur"]) + DMA_FIXED
            else:
                done[i] = st + op["dur"]
            op["t0"] = st
            order.append(i)
            for j in succ[i]:
                same = (ops[j]["eng"] == e) and op["key"] is None and ops[j]["key"] is None
                if i not in ops[j]["deps"]:
                    r = st + op["dur"]
                else:
                    r = done[i] + (0.05 if same else LAT)
                if r > rtime[j]:
                    rtime[j] = r
                indeg[j] -= 1
                if indeg[j] == 0:
                    ready[ops[j]["eng"]].append(j)
        self.est_makespan = max(done) if n else 0.0
        return order

    def emit(self, nc, block, eng_sems, key_sems, final_keys, order=None):
        ops = self.ops
        n = len(ops)
        if order is None:
            order = list(range(n))
        kept = self.kept_deps()
        signals = [False] * n
        for i in range(n):
            for d in kept[i]:
                signals[d] = True
        cnt = {e: 0 for e in self.COMPUTE}
        cum = {}
        sigval = [0] * n
        semof = [None] * n
        for i in order:
            op = ops[i]
            if op["key"] is not None:
                k = op["key"]
                cum[k] = cum.get(k, 0) + 16 * op["ndma"]
                sigval[i] = cum[k]
                semof[i] = key_sems[k]
            else:
                if signals[i]:
                    cnt[op["eng"]] += 1
                sigval[i] = cnt[op["eng"]]
                semof[i] = eng_sems[op["eng"]]
        for e, c in cnt.items():
            assert c < 30000, (e, c)
        by_eng = {e: [] for e in ("pe", "act", "dve", "pool", "sp")}
        pos = {}
        for p_, i in enumerate(order):
            by_eng[ops[i]["eng"]].append(i)
            pos[i] = p_
        for i in range(n):
            for d in ops[i]["deps"]:
                assert pos[d] < pos[i], "order is not topological"

        def run(engname, e):
            waited = {}
            for i in by_eng[engname]:
                op = ops[i]
                need = {}
                for d in kept[i]:
                    s = semof[d]
                    sid = id(s)
                    if sigval[d] > need.get(sid, (None, 0))[1]:
                        need[sid] = (s, sigval[d])
                for sid, (s, val) in need.items():
                    if waited.get(sid, 0) >= val:
                        continue
                    e.wait_ge(s, val)
                    waited[sid] = val
                res = op["fn"](e)
                if op["key"] is not None:
                    assert len(res) == op["ndma"]
                    for ins in res:
                        ins.then_inc(semof[i], 16)
                elif signals[i]:
                    res.then_inc(semof[i], 1)
            if engname == "sp":
                for k in final_keys:
                    e.wait_ge(key_sems[k], cum[k])

        @block.tensor
        def _(e):
            run("pe", e)

        @block.scalar
        def _(e):
            run("act", e)

        @block.vector
        def _(e):
            run("dve", e)

        @block.gpsimd
        def _(e):
            run("pool", e)

        @block.sync
        def _(e):
            run("sp", e)


def weight_block_plan():
    plan = []

    def ada(js):
        for j in js:
            plan.append((("ada", j), "w_ada", 8, 512 * j, 512, "A"))

    for g in range(NG):
        if g == 0:
            ada(range(0, 4))
        plan.append((("P", g, 0), "w_in", 8, 0, 512, "A"))
        plan.append((("WP", g), "w_pool", 8, 0, 256, "A"))
        plan.append((("ZA", g, 0), "w_in", 8, 4096, 512, "A"))
        plan.append((("P", g, 1), "w_in", 8, 512, 512, "A"))
        plan.append((("ZA", g, 1), "w_in", 8, 4096 + 512, 512, "A"))
        if g == 0:
            ada(range(4, 6))
        for hb in range(2):
            plan.append((("UX", g, hb), "w_in", 8, 1024 + 512 * hb, 512, "A"))
            plan.append((("UC", g, hb), "w_in", 8, 3072 + 512 * hb, 512, "A"))
            plan.append((("UB", g, hb), "w_in", 8, 2048 + 512 * hb, 512, "A"))
        if g == 0:
            ada(range(6, 10))
        for hb in range(2):
            plan.append((("ZB", g, hb), "w_in", 8, 5120 + 512 * hb, 512, "A"))
            plan.append((("BO", g, hb), "w_bout", 8, 512 * hb, 512, "A"))
        if g == 0:
            ada(range(10, 12))
        for hb in range(2):
            plan.append((("WO", g, hb), "w_o", 8, 512 * hb, 512, "A"))
        for fb in range(6):
            wd = 512 if fb < 5 else 256
            plan.append((("FG", g, fb), "w_up", 8, 512 * fb, wd, "A"))
            plan.append((("FV", g, fb), "w_up", 8, DFF + 512 * fb, wd, "A"))
        for c in range(NCH):
            plan.append((("DN", g, c), "w_down", NFF, 128 * c, 128, "A"))
    return plan


def build_program():
    nc = bass.Bass("TRN2", target_bir_lowering=False)
    dr = {}
    dr["xT"] = nc.dram_tensor("xT", [NG, D, NT], F32, kind="ExternalInput").ap()
    dr["tab"] = nc.dram_tensor("tab", [NG, 128, 5 * SPC], F32, kind="ExternalInput").ap()
    dr["prm"] = nc.dram_tensor("prm", [128, NPRM], F32, kind="ExternalInput").ap()
    dr["cT"] = nc.dram_tensor("cT", [128, 16], F32, kind="ExternalInput").ap()
    dr["w_ada"] = nc.dram_tensor("w_ada", [D, 6 * D], F32, kind="ExternalInput").ap()
    dr["w_in"] = nc.dram_tensor("w_in", [D, 6 * D], F32, kind="ExternalInput").ap()
    dr["w_pool"] = nc.dram_tensor("w_pool", [D, 256], F32, kind="ExternalInput").ap()
    dr["w_bout"] = nc.dram_tensor("w_bout", [D, D], F32, kind="ExternalInput").ap()
    dr["w_o"] = nc.dram_tensor("w_o", [D, D], F32, kind="ExternalInput").ap()
    dr["w_up"] = nc.dram_tensor("w_up", [D, 2 * DFF], F32, kind="ExternalInput").ap()
    dr["w_down"] = nc.dram_tensor("w_down", [DFF, D], F32, kind="ExternalInput").ap()
    yT = nc.dram_tensor("yT", [NG, D, GT], F32, kind="ExternalOutput").ap()

    XB = XSLOTS * NCH * TS * 4
    HB = NCH * NT * 2
    GB = NFF * NT * 2
    SB = 50816
    SMB = 4096

    from contextlib import ExitStack
    with ExitStack() as es:
        tX = es.enter_context(nc.sbuf_tensor("aX", [128, XB // 2], BF16))
        tH = es.enter_context(nc.sbuf_tensor("aH", [128, HB // 2], BF16))
        tG = es.enter_context(nc.sbuf_tensor("aG", [128, GB // 2], BF16))
        tS = es.enter_context(nc.sbuf_tensor("aS", [128, SB // 2], BF16))
        tM = es.enter_context(nc.sbuf_tensor("aM", [128, SMB // 2], BF16))
        tR = [es.enter_context(nc.sbuf_tensor(f"aR{s}", [128, SLOT_ELEMS], BF16)) for s in range(NSLOT)]
        tP = [es.enter_context(nc.psum_tensor(f"ps{b}", [128, 512], F32)) for b in range(8)]

        eng_sems = {e: es.enter_context(nc.semaphore(f"s_{e}")) for e in Prog.COMPUTE}
        key_names = [f"ring{s}" for s in range(NSLOT)] + ["small"] + [f"tab{g}" for g in range(NG)]
        nio = NTILE if TILE_IO else NCH
        key_names += [f"x{g}_{n}" for g in range(NG) for n in range(nio)]
        key_names += [f"st{g}_{n}" for g in range(NG) for n in range(nio)]
        key_sems = {k: es.enter_context(nc.semaphore(f"k_{k}")) for k in key_names}
        final_keys = [f"st{g}_{n}" for g in range(NG) for n in range(nio)]
        block = es.enter_context(nc.Block())

        P = Prog()

        X = Buf(tX, "X", 0, F32, XSLOTS * NCH * TS)
        cur_g = [0]
        H = Buf(tH, "H", 0, BF16, NCH * NT)
        MRG = Buf(tG, "G", 0, BF16, NCH * NT)
        Q = Buf(tG, "G", HB, BF16, NCH * NT)
        GBUF = Buf(tG, "G", 0, BF16, NFF * NT)

        def xslot(g, n):
            return (NTILE * g + n) % XSLOTS

        def xv(c, lo, hi):
            n = lo // TS
            assert hi <= (n + 1) * TS, (lo, hi)
            base = (xslot(cur_g[0], n) * NCH + c) * TS
            return X.v(base + lo - n * TS, base + hi - n * TS)

        def xtile3(g, n, lo, hi):
            sl = xslot(g, n)
            return X.v(sl * NCH * TS, (sl + 1) * NCH * TS).ap.rearrange("p (c t) -> p c t", c=NCH)[:, :, lo:hi]

        def hv(c, n):
            return H.v(c * NT + n * TS, c * NT + (n + 1) * TS)

        so = [0]

        def small(dtype, ncols):
            es_ = 4 if dtype == F32 else 2
            b = Buf(tM, "M", so[0], dtype, ncols)
            so[0] += (ncols * es_ + 3) // 4 * 4
            assert so[0] <= SMB
            return b

        PRM = small(F32, NPRM)
        MOD = small(F32, 48)
        DER = small(F32, 32)
        CTF = small(F32, 16)
        CTB = small(BF16, 16)
        ONES = small(BF16, 128)
        EPSB = small(F32, 1)
        TAB = [small(F32, 5 * SPC) for _ in range(NG)]

        def prm(col):
            return PRM.v(col, col + 1)

        def der(col):
            return DER.v(col, col + 1)

        def sbuf(off, dtype, ncols):
            es_ = 4 if dtype == F32 else 2
            assert off + ncols * es_ <= SB, (off, ncols)
            return Buf(tS, "S", off, dtype, ncols)

        pctr = [0]

        def psum(ncols=TS):
            b = pctr[0] % 8
            pctr[0] += 1
            return V(tP[b][:, 0:ncols], (f"ps{b}", 0, ncols * 4))

        def sc_ap(x):
            return x.ap if isinstance(x, V) else x

        def sc_iv(*xs):
            return [x.iv for x in xs if isinstance(x, V)]

        def ncols(v):
            return int(v.ap.shape[-1])

        def dve_dur(out):
            return (ncols(out) + 151) / 960.0

        def act(out, in_, func, bias=0.0, scale=1.0):
            d = 0.13 + ncols(out) * 0.00083 + (0.1 if in_.iv[0].startswith("ps") else 0.0) + \
                (0.12 if isinstance(scale, V) else 0.0)
            P.add("act", lambda e: e.activation(out.ap, in_.ap, func, bias=sc_ap(bias), scale=sc_ap(scale)),
                  [in_.iv] + sc_iv(bias, scale), [out.iv], dur=d)

        def tt(eng, out, a, b, op):
            P.add(eng, lambda e: e.tensor_tensor(out.ap, a.ap, b.ap, op), [a.iv, b.iv], [out.iv], dur=dve_dur(out))

        def ts(eng, out, a, s1, s2, op0, op1=None):
            if op1 is None:
                P.add(eng, lambda e: e.tensor_scalar(out.ap, a.ap, sc_ap(s1), None, op0),
                      [a.iv] + sc_iv(s1), [out.iv], dur=dve_dur(out))
            else:
                P.add(eng, lambda e: e.tensor_scalar(out.ap, a.ap, sc_ap(s1), sc_ap(s2), op0, op1),
                      [a.iv] + sc_iv(s1, s2), [out.iv], dur=dve_dur(out))

        def stt(eng, out, in0, scalar, in1, op0, op1):
            P.add(eng, lambda e: e.scalar_tensor_tensor(out.ap, in0.ap, sc_ap(scalar), in1.ap, op0, op1),
                  [in0.iv, in1.iv] + sc_iv(scalar), [out.iv], dur=dve_dur(out) + 0.06)

        def tt_multi(out_ap, out_ivs, a_ap, a_ivs, b_ap, b_ivs, op, nelem):
            P.add("dve", lambda e: e.tensor_tensor(out_ap, a_ap, b_ap, op), list(a_ivs) + list(b_ivs), list(out_ivs),
                  dur=(nelem + 151) / 960.0)

        def bcast8(v):
            return v.ap.rearrange("p (o t) -> p o t", o=1).broadcast_to([128, NCH, TS])

        def copy(eng, out, in_):
            if eng == "act":
                act(out, in_, AF.Copy)
            else:
                P.add(eng, lambda e: e.tensor_copy(out.ap, in_.ap), [in_.iv], [out.iv], dur=dve_dur(out))

        def memset(eng, out, val):
            P.add(eng, lambda e: e.memset(out.ap, val), [], [out.iv], dur=0.1)

        def mmgroup(out, pairs):
            def fn(e):
                last = None
                for i, (l, r) in enumerate(pairs):
                    last = e.matmul(out.ap, l.ap, r.ap, start=(i == 0), stop=(i == len(pairs) - 1))
                return last
            reads = []
            for l, r in pairs:
                reads.append(l.iv)
                reads.append(r.iv)
            P.add("pe", fn, reads, [out.iv], dur=len(pairs) * (ncols(out) * 0.000417 + 0.012))

        def dma(queue, key, pairs, reads, writes):
            def fn(e):
                return [e.dma_start(out=o, in_=i) for (o, i) in pairs]
            nbytes = 0
            for (o, i) in pairs:
                m = 4
                for d_ in i.shape:
                    m *= int(d_)
                nbytes += m
            P.add(queue, fn, reads, writes, key=key, ndma=len(pairs),
                  dur=(1.05 if queue == "pool" else 0.2) * len(pairs), xfer=2.0 + nbytes / 330e3)

        plan = weight_block_plan()
        issued = [False] * len(plan)
        cursor = [0]

        slot_of = {}
        free_slots = list(range(NSLOT))
        next_issue = [0]

        class Blk:
            def __init__(self, idx):
                self.idx = idx
                self.slot = slot_of[idx]
                tag, wname, nk, col0, wd, kind = plan[idx]
                self.nk, self.wd = nk, wd

            def lhsT(self, k, j0):
                lo = k * self.wd + j0
                return V(tR[self.slot][:, lo:lo + 128], (f"R{self.slot}", lo * 2, (lo + 128) * 2))

        def issue_more():
            while free_slots and next_issue[0] < len(plan):
                idx = next_issue[0]
                next_issue[0] += 1
                s = free_slots.pop(0)
                slot_of[idx] = s
                issued[idx] = True
                tag, wname, nk, col0, wd, kind = plan[idx]
                src = dr[wname][:, col0:col0 + wd].rearrange("(k p) j -> p k j", p=128)
                dst = tR[s][:, 0:nk * wd].rearrange("p (k j) -> p k j", k=nk)
                pairs = [(dst[:, k0:min(k0 + 8, nk), :], src[:, k0:min(k0 + 8, nk), :]) for k0 in range(0, nk, 8)]
                rd = []
                dma("pool", f"ring{s}", pairs, rd, [(f"R{s}", 0, nk * wd * 2)])

        def issue(idx):
            issue_more()

        def acquire(tag):
            idx = cursor[0]
            assert plan[idx][0] == tag, (plan[idx][0], tag)
            assert issued[idx], tag
            cursor[0] += 1
            return Blk(idx)

        def release(blk):
            free_slots.append(blk.slot)
            issue_more()

        dma("sp", "small", [(PRM.v().ap, dr["prm"][:, :]), (CTF.v().ap, dr["cT"][:, :])], [], [PRM.v().iv, CTF.v().iv])
        memset("pool", ONES.v(), 1.0 / D)
        memset("pool", EPSB.v(), EPS)
        copy("dve", CTB.v(), CTF.v())

        def xvg(g, c, lo, hi):
            n = lo // TS
            assert hi <= (n + 1) * TS, (lo, hi)
            base = (xslot(g, n) * NCH + c) * TS
            return X.v(base + lo - n * TS, base + hi - n * TS)

        def load_tile(g, n):
            src = dr["xT"][g].rearrange("(c p) t -> p c t", p=128)[:, :, n * TS:(n + 1) * TS]
            dst = xtile3(g, n, 0, TS)
            dma("sp", f"x{g}_{n}", [(dst, src)], [], [xvg(g, c, n * TS, (n + 1) * TS).iv for c in range(NCH)])

        def load_tab(g):
            dma("sp", f"tab{g}", [(TAB[g].v().ap, dr["tab"][g])], [], [TAB[g].v().iv])

        def load_x(g):
            for n in range(NTILE):
                load_tile(g, n)
            load_tab(g)

        def store_y(g, c):
            dma("sp", f"st{g}_{c}", [(yT[g][128 * c:128 * (c + 1), :], xv(c, HALO, NT).ap)], [xv(c, HALO, NT).iv], [])

        def store_chunk_tile(g, n, c):
            lo = HALO if n == 0 else n * TS
            hi = (n + 1) * TS
            dma("sp", f"st{g}_{n}", [(yT[g][128 * c:128 * (c + 1), lo - HALO:hi - HALO], xv(c, lo, hi).ap)],
                [xv(c, lo, hi).iv], [])

        def store_tile(g, n):
            lo = HALO if n == 0 else n * TS
            hi = (n + 1) * TS
            src = xtile3(g, n, lo - n * TS, hi - n * TS)
            dst = yT[g].rearrange("(c p) t -> p c t", p=128)[:, :, lo - HALO:hi - HALO]
            dma("sp", f"st{g}_{n}", [(dst, src)], [xv(c, lo, hi).iv for c in range(NCH)], [])

        def ada_blocks(js):
            for j in js:
                blk = acquire(("ada", j))
                ps = psum(4)
                for mm in range(4):
                    o = V(ps.ap[:, mm:mm + 1], ps.iv)
                    mmgroup(o, [(blk.lhsT(k, 128 * mm), CTB.v(2 * k, 2 * k + 1)) for k in range(8)])
                tt("dve", MOD.v(4 * j, 4 * j + 4), ps, PRM.v(C_BADA + 4 * j, C_BADA + 4 * j + 4), ALU.add)
                release(blk)
            if 3 in js:
                stt("dve", DER.v(0, 8), MOD.v(8, 16), 1.0, PRM.v(C_GPRE1, C_GPRE1 + 8), ALU.add, ALU.mult)
            if 5 in js:
                tt("dve", DER.v(8, 16), MOD.v(16, 24), PRM.v(C_GPOST1, C_GPOST1 + 8), ALU.mult)
            if 9 in js:
                stt("dve", DER.v(16, 24), MOD.v(32, 40), 1.0, PRM.v(C_GPRE2, C_GPRE2 + 8), ALU.add, ALU.mult)
            if 11 in js:
                tt("dve", DER.v(24, 32), MOD.v(40, 48), PRM.v(C_GPOST2, C_GPOST2 + 8), ALU.mult)

        def prenorm_phase(acol, shcol, between=None, sq_all_act=False):
            RS = sbuf(11264, F32, NT)
            TT = [sbuf(15488 + 4224 * s, F32, NT) for s in range(3)]
            for c in range(NCH):
                if sq_all_act or c % 2 == 0:
                    act(H.v(c * NT, (c + 1) * NT), xv(c, 0, NT), AF.Square)
                else:
                    tt("dve", H.v(c * NT, (c + 1) * NT), xv(c, 0, NT), xv(c, 0, NT), ALU.mult)
            for n in range(NTILE):
                ps = psum()
                mmgroup(ps, [(ONES.v(), hv(c, n)) for c in range(NCH)])
                ts("dve", RS.v(n * TS, (n + 1) * TS), ps, EPS, None, ALU.add)
            act(RS.v(), RS.v(), AF.Ln)
            act(RS.v(), RS.v(), AF.Exp, scale=-0.5)
            if between is not None:
                between()
            for c in range(NCH):
                t_ = TT[c % 3]
                tt("dve", t_.v(), xv(c, 0, NT), RS.v(), ALU.mult)
                act(H.v(c * NT, (c + 1) * NT), t_.v(), AF.Identity, bias=MOD.v(shcol + c, shcol + c + 1),
                    scale=der(acol + c))

        def postnorm_group(OSBF, scol, hook=None):
            RS = sbuf(39424, F32, NT)
            for c in range(NCH):
                ov = OSBF.v(c * NT, (c + 1) * NT)
                if c % 2 == 0:
                    act(H.v(c * NT, (c + 1) * NT), ov, AF.Square)
                else:
                    tt("dve", H.v(c * NT, (c + 1) * NT), ov, ov, ALU.mult)
            for n in range(NTILE):
                ps = psum()
                mmgroup(ps, [(ONES.v(), hv(c, n)) for c in range(NCH)])
                ts("dve", RS.v(n * TS, (n + 1) * TS), ps, EPS, None, ALU.add)
            act(RS.v(), RS.v(), AF.Ln)
            act(RS.v(), RS.v(), AF.Exp, scale=-0.5)
            for c in range(NCH):
                ov = OSBF.v(c * NT, (c + 1) * NT)
                tt("dve", ov, ov, RS.v(), ALU.mult)
                xs = xv(c, 0, NT)
                stt("dve", xs, ov, der(scol + c), xs, ALU.mult, ALU.add)
                if hook is not None:
                    hook(c)

        def m1_phase(g):
            UW = 16 + NT
            U = [sbuf(4288 * s, F32, UW) for s in range(2)]
            A = [sbuf(8576 + 4288 * s, F32, UW) for s in range(2)]
            Bb = [sbuf(17152 + 4288 * s, F32, UW) for s in range(2)]
            PG = [[sbuf(25728 + 2112 * (2 * ps_ + kk), BF16, NT) for kk in range(2)] for ps_ in range(2)]
            SIG = [sbuf(34176 + 1408 * s, F32, TS) for s in range(2)]
            T48 = [sbuf(36992 + 192 * s, F32, SPC) for s in range(2)]
            for s in range(2):
                memset("pool", U[s].v(0, 16), 0.0)
                memset("pool", A[s].v(0, 16), 0.0)
                memset("pool", Bb[s].v(0, 16), 0.0)
            blk = {}

            def stA(c):
                hb, j = divmod(c, 4)
                if j == 0:
                    blk["P", hb] = acquire(("P", g, hb))
                pblk = blk["P", hb]
                s = c % 2
                pgp = c // 2
                w = 2 ** (pgp + 1)
                for n in range(NTILE):
                    ps = psum()
                    mmgroup(ps, [(pblk.lhsT(k, 128 * j), hv(k, n)) for k in range(8)])
                    copy("act", U[s].v(16 + n * TS, 16 + (n + 1) * TS), ps)
                if j == 3:
                    release(pblk)
                tt("dve", U[s].v(16, 16 + SPC), U[s].v(16, 16 + SPC), TAB[g].v(0, SPC), ALU.mult)
                src = U[s]
                for st in range(pgp + 1):
                    dst = A[s] if st % 2 == 0 else Bb[s]
                    sh = 2 ** st
                    tt("dve", dst.v(16, 16 + NT), src.v(16, 16 + NT),
                       src.v(16 - sh, 16 - sh + NT), ALU.add)
                    src = dst
                pg = PG[pgp % 2][c % 2]
                stt("dve", pg.v(SPC, NT), src.v(16 + SPC, 16 + NT), 1.0 / w, U[s].v(16 + SPC, 16 + NT),
                    ALU.mult, ALU.subtract)
                tt("dve", T48[s].v(), src.v(16, 16 + SPC), TAB[g].v((1 + pgp) * SPC, (2 + pgp) * SPC), ALU.mult)
                tt("dve", pg.v(0, SPC), T48[s].v(), U[s].v(16, 16 + SPC), ALU.subtract)

            def stB(pgp):
                hb = pgp // 2
                if pgp == 0:
                    blk["WP"] = acquire(("WP", g))
                if pgp % 2 == 0:
                    blk["ZA", hb] = acquire(("ZA", g, hb))
                wp = blk["WP"]
                zblk = blk["ZA", hb]
                for mm in range(2):
                    co = 2 * pgp + mm
                    jz = co - 4 * hb
                    for n in range(NTILE):
                        psz = psum()
                        mmgroup(psz, [(zblk.lhsT(k, 128 * jz), hv(k, n)) for k in range(8)])
                        sg = SIG[(co * NTILE + n) % 2]
                        act(sg.v(), psz, AF.Sigmoid)
                        psy = psum()
                        mmgroup(psy, [(wp.lhsT(2 * pgp + kk, 128 * mm),
                                       PG[pgp % 2][kk].v(n * TS, (n + 1) * TS)) for kk in range(2)])
                        stt("dve", MRG.v(co * NT + n * TS, co * NT + (n + 1) * TS), psy, prm(C_PSC + co),
                            sg.v(), ALU.mult, ALU.mult)
                if pgp % 2 == 1:
                    release(zblk)
                if pgp == 3:
                    release(wp)

            for p in range(4):
                stA(2 * p)
                stA(2 * p + 1)
                if p >= 1:
                    stB(p - 1)
            stB(3)

        def m2a_phase(g):
            per = 16912
            UXS = [sbuf(per * s, F32, NT) for s in range(2)]
            UBS = [sbuf(per * s + 4224, F32, NT) for s in range(2)]
            VV = [sbuf(per * s + 8448, F32, 2 + NT) for s in range(2)]
            ACC = [sbuf(per * s + 12688, F32, NT) for s in range(2)]
            for s in range(2):
                memset("pool", VV[s].v(0, 2), 0.0)
            blk = {}

            def stA(c):
                hb, j = divmod(c, 4)
                if j == 0:
                    blk[hb] = (acquire(("UX", g, hb)), acquire(("UC", g, hb)), acquire(("UB", g, hb)))
                bx, bc, bb = blk[hb]
                s = c % 2
                for n in range(NTILE):
                    ps = psum()
                    mmgroup(ps, [(bx.lhsT(k, 128 * j), hv(k, n)) for k in range(8)])
                    copy("act", UXS[s].v(n * TS, (n + 1) * TS), ps)
                for n in range(NTILE):
                    ps = psum()
                    mmgroup(ps, [(bc.lhsT(k, 128 * j), hv(k, n)) for k in range(8)])
                    tt("dve", VV[s].v(2 + n * TS, 2 + (n + 1) * TS), ps, UXS[s].v(n * TS, (n + 1) * TS), ALU.mult)
                for n in range(NTILE):
                    ps = psum()
                    mmgroup(ps, [(bb.lhsT(k, 128 * j), hv(k, n)) for k in range(8)])
                    copy("act", UBS[s].v(n * TS, (n + 1) * TS), ps)
                tt("dve", VV[s].v(2, 2 + SPC), VV[s].v(2, 2 + SPC), TAB[g].v(0, SPC), ALU.mult)
                if j == 3:
                    release(bx)
                    release(bc)
                    release(bb)

            def stB(c):
                s = c % 2
                act(ACC[s].v(), VV[s].v(2, 2 + NT), AF.Identity, bias=prm(C_CB + c), scale=prm(C_CW + 16 + c))
                stt("dve", ACC[s].v(), VV[s].v(1, 1 + NT), prm(C_CW + 8 + c), ACC[s].v(), ALU.mult, ALU.add)
                stt("dve", ACC[s].v(), VV[s].v(0, NT), prm(C_CW + c), ACC[s].v(), ALU.mult, ALU.add)
                tt("dve", Q.v(c * NT, (c + 1) * NT), ACC[s].v(), UBS[s].v(), ALU.mult)

            for c in range(NCH + 1):
                if c < NCH:
                    stA(c)
                if c >= 1:
                    stB(c - 1)

        def m2b_phase(g):
            SIG = [sbuf(1408 * s, F32, TS) for s in range(2)]
            TMP = [sbuf(2816 + 1408 * s, F32, TS) for s in range(2)]
            blks = []
            for hb in range(2):
                bz = acquire(("ZB", g, hb))
                bo = acquire(("BO", g, hb))
                blks.append((bz, bo))
            for n in range(NTILE):
                for c in range(NCH):
                    hb, j = divmod(c, 4)
                    bz, bo = blks[hb]
                    s = (c * NTILE + n) % 2
                    psz = psum()
                    mmgroup(psz, [(bz.lhsT(k, 128 * j), hv(k, n)) for k in range(8)])
                    act(SIG[s].v(), psz, AF.Sigmoid)
                    psy = psum()
                    mmgroup(psy, [(bo.lhsT(k, 128 * j), Q.v(k * NT + n * TS, k * NT + (n + 1) * TS))
                                  for k in range(8)])
                    tt("dve", TMP[s].v(), psy, SIG[s].v(), ALU.mult)
                    mv = MRG.v(c * NT + n * TS, c * NT + (n + 1) * TS)
                    tt("dve", mv, TMP[s].v(), mv, ALU.add)
            for (bz, bo) in blks:
                release(bz)
                release(bo)

        def m3_phase(g):
            OSBF = sbuf(0, F32, NCH * NT)
            wo = [acquire(("WO", g, hb)) for hb in range(2)]
            for n in range(NTILE):
                for c in range(NCH):
                    ps = psum()
                    mmgroup(ps, [(wo[c // 4].lhsT(k, 128 * (c % 4)), MRG.v(k * NT + n * TS, k * NT + (n + 1) * TS))
                                 for k in range(8)])
                    copy("dve" if c % 2 == 0 else "act", OSBF.v(c * NT + n * TS, c * NT + (n + 1) * TS), ps)
            for b in wo:
                release(b)
            postnorm_group(OSBF, 8)

        def m3n_phase(g):
            OSB = [sbuf(SB - 11264, F32, NCH * TS), sbuf(25440, F32, NCH * TS)]
            gtail = 2 * HB

            def gbuf(off, ncols):
                assert gtail + off + ncols * 4 <= GB
                return Buf(tG, "G", gtail + off, F32, ncols)

            RS = [gbuf(1408 * s, TS) for s in range(2)]
            RS2 = [gbuf(2816 + 1408 * s, TS) for s in range(2)]
            TT = [gbuf(5632 + 1408 * s, TS) for s in range(4)]
            wo = [acquire(("WO", g, hb)) for hb in range(2)]
            for n in range(NTILE):
                s = n % 2
                for c in range(NCH):
                    ps = psum()
                    mmgroup(ps, [(wo[c // 4].lhsT(k, 128 * (c % 4)), MRG.v(k * NT + n * TS, k * NT + (n + 1) * TS))
                                 for k in range(8)])
                    copy("act", OSB[s].v(c * TS, (c + 1) * TS), ps)
                if n == NTILE - 1:
                    for b in wo:
                        release(b)
                norm_chain_tile(n, OSB[s], RS[s], RS2[s], TT, 8, 16, 24)

        def postnorm_tile(n, ovf, RSt, scol, hook=None, ov_all=None):
            for c in range(NCH):
                ov = ovf(c)
                if c % SQ_DVE_POST != SQ_DVE_POST - 1:
                    act(hv(c, n), ov, AF.Square)
                else:
                    tt("dve", hv(c, n), ov, ov, ALU.mult)
            ps = psum()
            mmgroup(ps, [(ONES.v(), hv(c, n)) for c in range(NCH)])
            act(RSt.v(), ps, AF.Ln, bias=EPSB.v())
            act(RSt.v(), RSt.v(), AF.Exp, scale=-0.5)
            if ov_all is not None:
                ap3, ivs = ov_all
                tt_multi(ap3, ivs, ap3, ivs, bcast8(RSt.v()), [RSt.v().iv], ALU.mult, NCH * TS)
            for c in range(NCH):
                ov = ovf(c)
                if ov_all is None:
                    tt("dve", ov, ov, RSt.v(), ALU.mult)
                xs = xv(c, n * TS, (n + 1) * TS)
                stt("dve", xs, ov, der(scol + c), xs, ALU.mult, ALU.add)
                if hook is not None:
                    hook(c)

        def prenorm_tile(n, RS2t, TT, acol, shcol):
            for c in range(NCH):
                xs = xv(c, n * TS, (n + 1) * TS)
                if c % SQ_DVE_PRE != SQ_DVE_PRE - 1:
                    act(hv(c, n), xs, AF.Square)
                else:
                    tt("dve", hv(c, n), xs, xs, ALU.mult)
            ps = psum()
            mmgroup(ps, [(ONES.v(), hv(c, n)) for c in range(NCH)])
            act(RS2t.v(), ps, AF.Ln, bias=EPSB.v())
            act(RS2t.v(), RS2t.v(), AF.Exp, scale=-0.5)
            big = isinstance(TT, Buf)
            if big:
                x3 = xtile3(cur_g[0], n, 0, TS)
                xivs = [xv(c, n * TS, (n + 1) * TS).iv for c in range(NCH)]
                t3 = TT.v().ap.rearrange("p (c t) -> p c t", c=NCH)
                tt_multi(t3, [TT.v().iv], x3, xivs, bcast8(RS2t.v()), [RS2t.v().iv], ALU.mult, NCH * TS)
            for c in range(NCH):
                if big:
                    t_ = Buf(TT.t, TT.arena, TT.off + c * TS * 4, F32, TS)
                else:
                    t_ = TT[c % len(TT)]
                    tt("dve", t_.v(), xv(c, n * TS, (n + 1) * TS), RS2t.v(), ALU.mult)
                if c % 4 == 3:
                    ts("dve", hv(c, n), t_.v(), der(acol + c), MOD.v(shcol + c, shcol + c + 1), ALU.mult, ALU.add)
                else:
                    act(hv(c, n), t_.v(), AF.Identity, bias=MOD.v(shcol + c, shcol + c + 1), scale=der(acol + c))

        def norm_chain_tile(n, OSBt, RSt, RS2t, TT, scol, acol, shcol):
            o3 = OSBt.v().ap.rearrange("p (c t) -> p c t", c=NCH)
            postnorm_tile(n, lambda c: OSBt.v(c * TS, (c + 1) * TS), RSt, scol, ov_all=(o3, [OSBt.v().iv]))
            prenorm_tile(n, RS2t, OSBt if BATCH_PRE else TT, acol, shcol)

        def prenorm1_tiles(g):
            RS2 = [sbuf(36608 + 1408 * s, F32, TS) for s in range(2)]
            if BATCH_PRE:
                TT = sbuf(39424, F32, NCH * TS)
            else:
                TT = [sbuf(39424 + 1408 * s, F32, TS) for s in range(4)]
            for n in range(NTILE):
                prenorm_tile(n, RS2[n % 2], TT, 0, 0)

        def f1_phase(g):
            NRS = 4
            if g + 1 < NG:
                load_tile(g + 1, 0)
                load_tab(g + 1)
            RG = [sbuf(8480 * s, F32, 2 + NT) for s in range(NRS)]
            RV = [sbuf(8480 * s + 4240, F32, 2 + NT) for s in range(NRS)]
            AG = [sbuf(8480 * NRS + 8448 * s, F32, NT) for s in range(2)]
            AV = [sbuf(8480 * NRS + 8448 * s + 4224, F32, NT) for s in range(2)]
            for s in range(NRS):
                memset("pool", RG[s].v(0, 2), 0.0)
                memset("pool", RV[s].v(0, 2), 0.0)
            blk = {}

            def stA(f):
                fb, j = divmod(f, 4)
                if j == 0:
                    blk[fb] = (acquire(("FG", g, fb)), acquire(("FV", g, fb)))
                bg, bv = blk[fb]
                s = f % NRS
                for (b_, R) in ((bg, RG[s]), (bv, RV[s])):
                    for n in range(NTILE):
                        ps = psum()
                        mmgroup(ps, [(b_.lhsT(k, 128 * j), hv(k, n)) for k in range(8)])
                        copy("act", R.v(2 + n * TS, 2 + (n + 1) * TS), ps)
                    tt("dve", R.v(2, 2 + SPC), R.v(2, 2 + SPC), TAB[g].v(0, SPC), ALU.mult)
                if j == bg.wd // 128 - 1:
                    release(bg)
                    release(bv)

            def stB1(f):
                s = f % NRS
                a = f % 2
                for (R, Ab, ch) in ((RG[s], AG[a], f), (RV[s], AV[a], NFF + f)):
                    act(Ab.v(), R.v(2, 2 + NT), AF.Identity, bias=prm(C_FB + ch), scale=prm(C_FW + 88 + ch))
                    stt("dve", Ab.v(), R.v(1, 1 + NT), prm(C_FW + 44 + ch), Ab.v(), ALU.mult, ALU.add)
                    stt("dve", Ab.v(), R.v(0, NT), prm(C_FW + ch), Ab.v(), ALU.mult, ALU.add)

            def stB2(f):
                a = f % 2
                act(AG[a].v(), AG[a].v(), AF.Gelu_apprx_tanh)
                tt("dve", GBUF.v(f * NT, (f + 1) * NT), AG[a].v(), AV[a].v(), ALU.mult)

            for i in range(NFF + 2):
                if i < NFF:
                    stA(i)
                if 0 <= i - 1 < NFF:
                    stB1(i - 1)
                if 0 <= i - 2 < NFF:
                    stB2(i - 2)

        def f2_phase(g):
            OSBF = sbuf(0, F32, NCH * NT)
            FS = 14

            def mm_part(ps, bd, n, f0, f1):
                def fn(e):
                    last = None
                    for f in range(f0, f1):
                        last = e.matmul(ps.ap, bd.lhsT(f, 0).ap, GBUF.v(f * NT + n * TS, f * NT + (n + 1) * TS).ap,
                                        start=(f == 0), stop=(f == NFF - 1))
                    return last
                reads = []
                for f in range(f0, f1):
                    reads.append(bd.lhsT(f, 0).iv)
                    reads.append(GBUF.v(f * NT + n * TS, f * NT + (n + 1) * TS).iv)
                P.add("pe", fn, reads, [ps.iv], dur=(f1 - f0) * (TS * 0.000417 + 0.012))

            def evac(c, n, ps):
                copy("act" if (c * NTILE + n) % 2 == 0 else "dve",
                     OSBF.v(c * NT + n * TS, c * NT + (n + 1) * TS), ps)

            bds = [acquire(("DN", g, 0)), acquire(("DN", g, 1))]
            pss = {}
            for c in range(2):
                for n in range(NTILE):
                    pss[c, n] = psum()
                    mm_part(pss[c, n], bds[c], n, 0, FS)
            for c in range(2):
                for n in range(NTILE):
                    mm_part(pss[c, n], bds[c], n, FS, NFF)
                    evac(c, n, pss[c, n])
                release(bds[c])
            if not TILE_IO:
                for c in range(2, NCH):
                    bd = acquire(("DN", g, c))
                    for n in range(NTILE):
                        ps = psum()
                        mm_part(ps, bd, n, 0, NFF)
                        evac(c, n, ps)
                    release(bd)
                postnorm_group(OSBF, 24, hook=lambda c: store_y(g, c))
                return
            for c in range(2, NCH - TAILC):
                bd = acquire(("DN", g, c))
                for n in range(NTILE):
                    ps = psum()
                    mm_part(ps, bd, n, 0, NFF)
                    evac(c, n, ps)
                release(bd)
            def osbf_all(n):
                ap3 = OSBF.v().ap.rearrange("p (c t) -> p c t", c=NCH)[:, :, n * TS:(n + 1) * TS]
                return (ap3, [OSBF.v(c * NT + n * TS, c * NT + (n + 1) * TS).iv for c in range(NCH)])

            tail = {c: acquire(("DN", g, c)) for c in range(NCH - TAILC, NCH)}
            RS = [sbuf(33792 + 1408 * s_, F32, TS) for s_ in range(2)]
            for n in range(NTILE):
                for c in range(NCH - TAILC, NCH):
                    ps = psum()
                    mm_part(ps, tail[c], n, 0, NFF)
                    evac(c, n, ps)
                if n == NTILE - 1:
                    for c in range(NCH - TAILC, NCH):
                        release(tail[c])
                if n == NTILE - 1:
                    postnorm_tile(n, lambda c, n=n: OSBF.v(c * NT + n * TS, c * NT + (n + 1) * TS), RS[n % 2], 24,
                                  hook=lambda c, n=n: store_chunk_tile(g, n, c), ov_all=osbf_all(n))
                else:
                    postnorm_tile(n, lambda c, n=n: OSBF.v(c * NT + n * TS, c * NT + (n + 1) * TS), RS[n % 2], 24,
                                  ov_all=osbf_all(n))
                    store_tile(g, n)
                    if g + 1 < NG:
                        load_tile(g + 1, n + 1)

        load_x(0)
        for s_ in range(NSLOT):
            issue(s_)
        for g in range(NG):
            cur_g[0] = g
            if TILE_IO:
                if g == 0:
                    ada_blocks(range(0, 4))
                prenorm1_tiles(g)
            else:
                prenorm_phase(0, 0, between=(lambda: ada_blocks(range(0, 4))) if g == 0 else None)
            m1_phase(g)
            if g == 0:
                ada_blocks(range(4, 6))
            m2a_phase(g)
            if g == 0:
                ada_blocks(range(6, 10))
            m2b_phase(g)
            if g == 0:
                ada_blocks(range(10, 12))
            if TILE_CHAIN:
                m3n_phase(g)
            else:
                m3_phase(g)
                prenorm_phase(16, 24, sq_all_act=True)
            f1_phase(g)
            f2_phase(g)
        assert cursor[0] == len(plan), (cursor[0], len(plan))

        order = P.schedule() if USE_SCHED else None
        P.emit(nc, block, eng_sems, key_sems, final_keys, order)
    return nc


def _cols(v, n):
    return np.ascontiguousarray(np.asarray(v, np.float32).reshape(n, 128).T)


def _host_inputs(inputs):
    x = np.asarray(inputs["x"], np.float32)
    c = np.asarray(inputs["c"], np.float32)
    prm = np.zeros((128, NPRM), np.float32)
    prm[:, C_GPRE1:C_GPRE1 + 8] = _cols(inputs["g_pre_mix"][0], 8)
    prm[:, C_GPOST1:C_GPOST1 + 8] = _cols(inputs["g_post_mix"][0], 8)
    prm[:, C_GPRE2:C_GPRE2 + 8] = _cols(inputs["g_pre_ffn"][0], 8)
    prm[:, C_GPOST2:C_GPOST2 + 8] = _cols(inputs["g_post_ffn"][0], 8)
    prm[:, C_BADA:C_BADA + 48] = _cols(inputs["b_ada"][0], 48)
    prm[:, C_PSC:C_PSC + 8] = _cols(inputs["pool_scale"][0], 8)
    for t in range(3):
        prm[:, C_CW + 8 * t:C_CW + 8 * t + 8] = _cols(inputs["conv_w"][0][t], 8)
        prm[:, C_FW + 44 * t:C_FW + 44 * t + 44] = _cols(inputs["ffn_conv_w"][0][t], 44)
    prm[:, C_CB:C_CB + 8] = _cols(inputs["conv_b"][0], 8)
    prm[:, C_FB:C_FB + 44] = _cols(inputs["ffn_conv_b"][0], 44)
    shared = {
        "prm": prm,
        "w_ada": np.ascontiguousarray(np.asarray(inputs["w_ada"], np.float32)[0]),
        "w_in": np.ascontiguousarray(np.asarray(inputs["w_in"], np.float32)[0]),
        "w_pool": np.ascontiguousarray(np.asarray(inputs["w_pool"], np.float32)[0].reshape(D, 256)),
        "w_bout": np.ascontiguousarray(np.asarray(inputs["w_bout"], np.float32)[0]),
        "w_o": np.ascontiguousarray(np.asarray(inputs["w_o"], np.float32)[0]),
        "w_up": np.ascontiguousarray(np.asarray(inputs["w_up"], np.float32)[0]),
        "w_down": np.ascontiguousarray(np.asarray(inputs["w_down"], np.float32)[0]),
    }
    windows = (2, 4, 8, 16)
    in_maps = []
    for i in range(N_CORES):
        b = i // 4
        t0 = (i % 4) * CORE_TOK
        xT = np.zeros((NG, D, NT), np.float32)
        tab = np.ones((NG, 128, 5, SPC), np.float32)
        for g in range(NG):
            start = t0 + g * GT - HALO
            lo = max(start, 0)
            xT[g][:, lo - start:] = x[b, lo:start + NT, :].T
            for p in range(SPC):
                t = start + p
                if t < 0:
                    tab[g, :, 0, p] = 0.0
                else:
                    for k, w in enumerate(windows):
                        tab[g, :, 1 + k, p] = 1.0 / min(t + 1, w)
        m = dict(shared)
        m["xT"] = xT
        m["tab"] = tab.reshape(NG, 128, 5 * SPC)
        cT = np.zeros((128, 16), np.float32)
        cT[:, 0::2] = _cols(c[b], 8)
        m["cT"] = cT
        in_maps.append(m)
    return in_maps


def kernel(**inputs):
    in_maps = _host_inputs(inputs)
    nc = build_program()
    res = run_bass_kernel_spmd(nc, in_maps, core_ids=list(range(N_CORES)))
    out = np.zeros((2, 8192, D), np.float32)
    for i in range(N_CORES):
        b = i // 4
        t0 = (i % 4) * CORE_TOK
        y = res.results[i]["yT"]
        for g in range(NG):
            out[b, t0 + g * GT:t0 + (g + 1) * GT, :] = y[g].T
    return out
```

```python
import numpy as np
import concourse.bass as bass
import concourse.mybir as mybir
from concourse.bass_utils import run_bass_kernel_spmd

F32 = mybir.dt.float32
BF16 = mybir.dt.bfloat16
AF = mybir.ActivationFunctionType
ALU = mybir.AluOpType

D = 1024
DFF = 2816
NCH = 8
NFF = 22
NG = 2
GT = 1024
HALO = 20
NT = GT + HALO
TS = 348
NTILE = NT // TS
SPC = 48
EPS = 1e-6
NSLOT = 6
SLOT_ELEMS = 4096
N_CORES = 8
USE_SCHED = True
TILE_CHAIN = True
TILE_IO = True
TAILC = 4
XSLOTS = 4
SQ_DVE_POST = 4
SQ_DVE_PRE = 8
CORE_TOK = 2048

C_GPRE1, C_GPOST1, C_GPRE2, C_GPOST2 = 0, 8, 16, 24
C_BADA = 32
C_PSC = 80
C_CW = 88
C_CB = 112
C_FW = 120
C_FB = 252
NPRM = 296


class V:
    __slots__ = ("ap", "iv")

    def __init__(self, ap, iv):
        self.ap = ap
        self.iv = iv


class Buf:
    def __init__(self, t, arena, off_b, dtype, ncols):
        self.t, self.arena, self.off, self.dtype, self.ncols = t, arena, off_b, dtype, ncols
        self.es = 4 if dtype == F32 else 2
        assert off_b % 4 == 0

    def v(self, lo=0, hi=None):
        if hi is None:
            hi = self.ncols
        assert 0 <= lo < hi <= self.ncols, (lo, hi, self.ncols)
        b0 = self.off + lo * self.es
        b1 = self.off + hi * self.es
        assert b0 % 2 == 0
        a = self.t[:, b0 // 2: b1 // 2]
        if self.dtype == F32:
            assert b0 % 4 == 0
            a = a.bitcast(F32)
        return V(a, (self.arena, b0, b1))


class Prog:
    COMPUTE = ("pe", "act", "dve", "pool")

    def __init__(self):
        self.ops = []
        self.recs = {}

    def add(self, eng, fn, reads, writes, key=None, ndma=0, dur=0.5, xfer=0.0):
        oid = len(self.ops)
        deps = {}
        isdma = key is not None
        for (a, lo, hi) in reads:
            for r in self.recs.setdefault(a, []):
                if r[3] and r[0] < hi and lo < r[1]:
                    deps[r[2]] = True
        for (a, lo, hi) in writes:
            for r in self.recs.setdefault(a, []):
                if r[0] < hi and lo < r[1]:
                    deps.setdefault(r[2], False)
        for (a, lo, hi) in writes:
            L = self.recs[a]
            L[:] = [r for r in L if not (lo <= r[0] and r[1] <= hi)]
            L.append([lo, hi, oid, True, eng, isdma])
        for (a, lo, hi) in reads:
            L = self.recs[a]
            L.append([lo, hi, oid, False, eng, isdma])
        deps.pop(oid, None)
        self.ops.append(dict(eng=eng, fn=fn, deps=deps, key=key, ndma=ndma, dur=dur, xfer=xfer))
        return oid

    def kept_deps(self):
        ops = self.ops
        n = len(ops)
        kept = [None] * n
        for i, op in enumerate(ops):
            kd = []
            for d, raw in op["deps"].items():
                dop = ops[d]
                d_dma = dop["key"] is not None
                o_dma = op["key"] is not None
                if not d_dma and not o_dma and dop["eng"] == op["eng"]:
                    if op["eng"] == "pe":
                        continue
                kd.append(d)
            kept[i] = kd
        return kept

    def schedule(self):
        ops = self.ops
        n = len(ops)
        LAT = 0.7
        succ = [[] for _ in range(n)]
        indeg = [0] * n
        last_sp = None
        for i, op in enumerate(ops):
            ds = set(op["deps"].keys())
            if op["eng"] == "sp":
                if last_sp is not None:
                    ds.add(last_sp)
                last_sp = i
            for d in ds:
                succ[d].append(i)
                indeg[i] += 1
        blev = [0.0] * n
        for i in range(n - 1, -1, -1):
            m = 0.0
            for j in succ[i]:
                if blev[j] > m:
                    m = blev[j]
            blev[i] = m + ops[i]["dur"] + ops[i]["xfer"] + LAT
        t_eng = {e: 7.0 for e in ("pe", "act", "dve", "pool", "sp")}
        pipe = [0.0]
        DMA_FIXED = 2.0
        ready = {e: [] for e in t_eng}
        rtime = [0.0] * n
        done = [0.0] * n
        for i in range(n):
            if indeg[i] == 0:
                ready[ops[i]["eng"]].append(i)
        order = []
        while len(order) < n:
            best = None
            for e, L in ready.items():
                if not L:
                    continue
                te = t_eng[e]
                bi = None
                bkey = None
                for i in L:
                    st = rtime[i] if rtime[i] > te else te
                    key = (st, -blev[i], i)
                    if bkey is None or key < bkey:
                        bkey = key
                        bi = i
                if best is None or bkey < best[0]:
                    best = (bkey, e, bi)
            (st, _, _), e, i = best
            ready[e].remove(i)
            op = ops[i]
            t_eng[e] = st + op["dur"]
            if op["key"] is not None:
                beg = max(st + op["dur"], pipe[0])
                pipe[0] = beg + op["xfer"] - DMA_FIXED
                done[i] = max(beg + op["xfer"] - DMA_FIXED, st + op["dur"]) + DMA_FIXED
            else:
                done[i] = st + op["dur"]
            op["t0"] = st
            order.append(i)
            for j in succ[i]:
                same = (ops[j]["eng"] == e) and op["key"] is None and ops[j]["key"] is None
                if i not in ops[j]["deps"]:
                    r = st + op["dur"]
                else:
                    r = done[i] + (0.05 if same else LAT)
                if r > rtime[j]:
                    rtime[j] = r
                indeg[j] -= 1
                if indeg[j] == 0:
                    ready[ops[j]["eng"]].append(j)
        self.est_makespan = max(done) if n else 0.0
        return order

    def emit(self, nc, block, eng_sems, key_sems, final_keys, order=None):
        ops = self.ops
        n = len(ops)
        if order is None:
            order = list(range(n))
        kept = self.kept_deps()
        signals = [False] * n
        for i in range(n):
            for d in kept[i]:
                signals[d] = True
        cnt = {e: 0 for e in self.COMPUTE}
        cum = {}
        sigval = [0] * n
        semof = [None] * n
        for i in order:
            op = ops[i]
            if op["key"] is not None:
                k = op["key"]
                cum[k] = cum.get(k, 0) + 16 * op["ndma"]
                sigval[i] = cum[k]
                semof[i] = key_sems[k]
            else:
                if signals[i]:
                    cnt[op["eng"]] += 1
                sigval[i] = cnt[op["eng"]]
                semof[i] = eng_sems[op["eng"]]
        for e, c in cnt.items():
            assert c < 30000, (e, c)
        by_eng = {e: [] for e in ("pe", "act", "dve", "pool", "sp")}
        pos = {}
        for p_, i in enumerate(order):
            by_eng[ops[i]["eng"]].append(i)
            pos[i] = p_
        for i in range(n):
            for d in ops[i]["deps"]:
                assert pos[d] < pos[i], "order is not topological"

        def run(engname, e):
            waited = {}
            for i in by_eng[engname]:
                op = ops[i]
                need = {}
                for d in kept[i]:
                    s = semof[d]
                    sid = id(s)
                    if sigval[d] > need.get(sid, (None, 0))[1]:
                        need[sid] = (s, sigval[d])
                for sid, (s, val) in need.items():
                    if waited.get(sid, 0) >= val:
                        continue
                    e.wait_ge(s, val)
                    waited[sid] = val
                res = op["fn"](e)
                if op["key"] is not None:
                    assert len(res) == op["ndma"]
                    for ins in res:
                        ins.then_inc(semof[i], 16)
                elif signals[i]:
                    res.then_inc(semof[i], 1)
            if engname == "sp":
                for k in final_keys:
                    e.wait_ge(key_sems[k], cum[k])

        @block.tensor
        def _(e):
            run("pe", e)

        @block.scalar
        def _(e):
            run("act", e)

        @block.vector
        def _(e):
            run("dve", e)

        @block.gpsimd
        def _(e):
            run("pool", e)

        @block.sync
        def _(e):
            run("sp", e)


def weight_block_plan():
    plan = []

    def ada(js):
        for j in js:
            plan.append((("ada", j), "w_ada", 8, 512 * j, 512, "A"))

    for g in range(NG):
        if g == 0:
            ada(range(0, 4))
        plan.append((("P", g, 0), "w_in", 8, 0, 512, "A"))
        plan.append((("WP", g), "w_pool", 8, 0, 256, "A"))
        plan.append((("ZA", g, 0), "w_in", 8, 4096, 512, "A"))
        plan.append((("P", g, 1), "w_in", 8, 512, 512, "A"))
        plan.append((("ZA", g, 1), "w_in", 8, 4096 + 512, 512, "A"))
        if g == 0:
            ada(range(4, 6))
        for hb in range(2):
            if g == 0 and hb == 1:
                ada(range(6, 10))
            plan.append((("UX", g, hb), "w_in", 8, 1024 + 512 * hb, 512, "A"))
            plan.append((("UC", g, hb), "w_in", 8, 3072 + 512 * hb, 512, "A"))
            plan.append((("UB", g, hb), "w_in", 8, 2048 + 512 * hb, 512, "A"))
        for hb in range(2):
            plan.append((("ZB", g, hb), "w_in", 8, 5120 + 512 * hb, 512, "A"))
            plan.append((("BO", g, hb), "w_bout", 8, 512 * hb, 512, "A"))
        if g == 0:
            ada(range(10, 12))
        for hb in range(2):
            plan.append((("WO", g, hb), "w_o", 8, 512 * hb, 512, "A"))
        for fb in range(6):
            wd = 512 if fb < 5 else 256
            plan.append((("FG", g, fb), "w_up", 8, 512 * fb, wd, "A"))
            plan.append((("FV", g, fb), "w_up", 8, DFF + 512 * fb, wd, "A"))
        for c in range(NCH):
            plan.append((("DN", g, c), "w_down", NFF, 128 * c, 128, "A"))
    return plan


def build_program():
    nc = bass.Bass("TRN2", target_bir_lowering=False)
    dr = {}
    dr["xT"] = nc.dram_tensor("xT", [NG, D, NT], F32, kind="ExternalInput").ap()
    dr["tab"] = nc.dram_tensor("tab", [NG, 128, 5 * SPC], F32, kind="ExternalInput").ap()
    dr["prm"] = nc.dram_tensor("prm", [128, NPRM], F32, kind="ExternalInput").ap()
    dr["cT"] = nc.dram_tensor("cT", [128, 16], F32, kind="ExternalInput").ap()
    dr["w_ada"] = nc.dram_tensor("w_ada", [D, 6 * D], F32, kind="ExternalInput").ap()
    dr["w_in"] = nc.dram_tensor("w_in", [D, 6 * D], F32, kind="ExternalInput").ap()
    dr["w_pool"] = nc.dram_tensor("w_pool", [D, 256], F32, kind="ExternalInput").ap()
    dr["w_bout"] = nc.dram_tensor("w_bout", [D, D], F32, kind="ExternalInput").ap()
    dr["w_o"] = nc.dram_tensor("w_o", [D, D], F32, kind="ExternalInput").ap()
    dr["w_up"] = nc.dram_tensor("w_up", [D, 2 * DFF], F32, kind="ExternalInput").ap()
    dr["w_down"] = nc.dram_tensor("w_down", [DFF, D], F32, kind="ExternalInput").ap()
    yT = nc.dram_tensor("yT", [NG, D, GT], F32, kind="ExternalOutput").ap()

    XB = XSLOTS * NCH * TS * 4
    HB = NCH * NT * 2
    GB = NFF * NT * 2
    SB = 50816
    SMB = 4096

    from contextlib import ExitStack
    with ExitStack() as es:
        tX = es.enter_context(nc.sbuf_tensor("aX", [128, XB // 2], BF16))
        tH = es.enter_context(nc.sbuf_tensor("aH", [128, HB // 2], BF16))
        tG = es.enter_context(nc.sbuf_tensor("aG", [128, GB // 2], BF16))
        tS = es.enter_context(nc.sbuf_tensor("aS", [128, SB // 2], BF16))
        tM = es.enter_context(nc.sbuf_tensor("aM", [128, SMB // 2], BF16))
        tR = [es.enter_context(nc.sbuf_tensor(f"aR{s}", [128, SLOT_ELEMS], BF16)) for s in range(NSLOT)]
        tP = [es.enter_context(nc.psum_tensor(f"ps{b}", [128, 512], F32)) for b in range(8)]

        eng_sems = {e: es.enter_context(nc.semaphore(f"s_{e}")) for e in Prog.COMPUTE}
        key_names = [f"ring{s}" for s in range(NSLOT)] + ["small"] + [f"tab{g}" for g in range(NG)]
        nio = NTILE if TILE_IO else NCH
        key_names += [f"x{g}_{n}" for g in range(NG) for n in range(nio)]
        key_names += [f"st{g}_{n}" for g in range(NG) for n in range(nio)]
        key_sems = {k: es.enter_context(nc.semaphore(f"k_{k}")) for k in key_names}
        final_keys = [f"st{g}_{n}" for g in range(NG) for n in range(nio)]
        block = es.enter_context(nc.Block())

        P = Prog()

        X = Buf(tX, "X", 0, F32, XSLOTS * NCH * TS)
        cur_g = [0]
        H = Buf(tH, "H", 0, BF16, NCH * NT)
        MRG = Buf(tG, "G", 0, BF16, NCH * NT)
        Q = Buf(tG, "G", HB, BF16, NCH * NT)
        GBUF = Buf(tG, "G", 0, BF16, NFF * NT)

        def xslot(g, n):
            return (NTILE * g + n) % XSLOTS

        def xv(c, lo, hi):
            n = lo // TS
            assert hi <= (n + 1) * TS, (lo, hi)
            base = (xslot(cur_g[0], n) * NCH + c) * TS
            return X.v(base + lo - n * TS, base + hi - n * TS)

        def xtile3(g, n, lo, hi):
            sl = xslot(g, n)
            return X.v(sl * NCH * TS, (sl + 1) * NCH * TS).ap.rearrange("p (c t) -> p c t", c=NCH)[:, :, lo:hi]

        def hv(c, n):
            return H.v(c * NT + n * TS, c * NT + (n + 1) * TS)

        so = [0]

        def small(dtype, ncols):
            es_ = 4 if dtype == F32 else 2
            b = Buf(tM, "M", so[0], dtype, ncols)
            so[0] += (ncols * es_ + 3) // 4 * 4
            assert so[0] <= SMB
            return b

        PRM = small(F32, NPRM)
        MOD = small(F32, 48)
        DER = small(F32, 32)
        CTF = small(F32, 16)
        CTB = small(BF16, 16)
        ONES = small(BF16, 128)
        EPSB = small(F32, 1)
        TAB = [small(F32, 5 * SPC) for _ in range(NG)]

        def prm(col):
            return PRM.v(col, col + 1)

        def der(col):
            return DER.v(col, col + 1)

        def sbuf(off, dtype, ncols):
            es_ = 4 if dtype == F32 else 2
            assert off + ncols * es_ <= SB, (off, ncols)
            return Buf(tS, "S", off, dtype, ncols)

        pctr = [0]

        def psum(ncols=TS):
            b = pctr[0] % 8
            pctr[0] += 1
            return V(tP[b][:, 0:ncols], (f"ps{b}", 0, ncols * 4))

        def sc_ap(x):
            return x.ap if isinstance(x, V) else x

        def sc_iv(*xs):
            return [x.iv for x in xs if isinstance(x, V)]

        def ncols(v):
            return int(v.ap.shape[-1])

        def dve_dur(out):
            return (ncols(out) + 151) / 960.0

        def act(out, in_, func, bias=0.0, scale=1.0):
            d = 0.13 + ncols(out) * 0.00083 + (0.1 if in_.iv[0].startswith("ps") else 0.0) + \
                (0.12 if isinstance(scale, V) else 0.0)
            P.add("act", lambda e: e.activation(out.ap, in_.ap, func, bias=sc_ap(bias), scale=sc_ap(scale)),
                  [in_.iv] + sc_iv(bias, scale), [out.iv], dur=d)

        def tt(eng, out, a, b, op):
            P.add(eng, lambda e: e.tensor_tensor(out.ap, a.ap, b.ap, op), [a.iv, b.iv], [out.iv], dur=dve_dur(out))

        def ts(eng, out, a, s1, s2, op0, op1=None):
            if op1 is None:
                P.add(eng, lambda e: e.tensor_scalar(out.ap, a.ap, sc_ap(s1), None, op0),
                      [a.iv] + sc_iv(s1), [out.iv], dur=dve_dur(out))
            else:
                P.add(eng, lambda e: e.tensor_scalar(out.ap, a.ap, sc_ap(s1), sc_ap(s2), op0, op1),
                      [a.iv] + sc_iv(s1, s2), [out.iv], dur=dve_dur(out))

        def stt(eng, out, in0, scalar, in1, op0, op1):
            P.add(eng, lambda e: e.scalar_tensor_tensor(out.ap, in0.ap, sc_ap(scalar), in1.ap, op0, op1),
                  [in0.iv, in1.iv] + sc_iv(scalar), [out.iv], dur=dve_dur(out) + 0.06)

        def copy(eng, out, in_):
            if eng == "act":
                act(out, in_, AF.Copy)
            else:
                P.add(eng, lambda e: e.tensor_copy(out.ap, in_.ap), [in_.iv], [out.iv], dur=dve_dur(out))

        def memset(eng, out, val):
            P.add(eng, lambda e: e.memset(out.ap, val), [], [out.iv], dur=0.1)

        def mmgroup(out, pairs):
            def fn(e):
                last = None
                for i, (l, r) in enumerate(pairs):
                    last = e.matmul(out.ap, l.ap, r.ap, start=(i == 0), stop=(i == len(pairs) - 1))
                return last
            reads = []
            for l, r in pairs:
                reads.append(l.iv)
                reads.append(r.iv)
            P.add("pe", fn, reads, [out.iv], dur=len(pairs) * (ncols(out) * 0.000417 + 0.012))

        def dma(queue, key, pairs, reads, writes):
            def fn(e):
                return [e.dma_start(out=o, in_=i) for (o, i) in pairs]
            nbytes = 0
            for (o, i) in pairs:
                m = 4
                for d_ in i.shape:
                    m *= int(d_)
                nbytes += m
            P.add(queue, fn, reads, writes, key=key, ndma=len(pairs),
                  dur=(1.05 if queue == "pool" else 0.2) * len(pairs), xfer=2.0 + nbytes / 330e3)

        plan = weight_block_plan()
        issued = [False] * len(plan)
        cursor = [0]

        slot_of = {}
        free_slots = list(range(NSLOT))
        next_issue = [0]

        class Blk:
            def __init__(self, idx):
                self.idx = idx
                self.slot = slot_of[idx]
                tag, wname, nk, col0, wd, kind = plan[idx]
                self.nk, self.wd = nk, wd

            def lhsT(self, k, j0):
                lo = k * self.wd + j0
                return V(tR[self.slot][:, lo:lo + 128], (f"R{self.slot}", lo * 2, (lo + 128) * 2))

        def issue_more():
            while free_slots and next_issue[0] < len(plan):
                idx = next_issue[0]
                next_issue[0] += 1
                s = free_slots.pop(0)
                slot_of[idx] = s
                issued[idx] = True
                tag, wname, nk, col0, wd, kind = plan[idx]
                src = dr[wname][:, col0:col0 + wd].rearrange("(k p) j -> p k j", p=128)
                dst = tR[s][:, 0:nk * wd].rearrange("p (k j) -> p k j", k=nk)
                pairs = [(dst[:, k0:min(k0 + 8, nk), :], src[:, k0:min(k0 + 8, nk), :]) for k0 in range(0, nk, 8)]
                rd = []
                dma("pool", f"ring{s}", pairs, rd, [(f"R{s}", 0, nk * wd * 2)])

        def issue(idx):
            issue_more()

        def acquire(tag):
            idx = cursor[0]
            assert plan[idx][0] == tag, (plan[idx][0], tag)
            assert issued[idx], tag
            cursor[0] += 1
            return Blk(idx)

        def release(blk):
            free_slots.append(blk.slot)
            issue_more()

        dma("sp", "small", [(PRM.v().ap, dr["prm"][:, :]), (CTF.v().ap, dr["cT"][:, :])], [], [PRM.v().iv, CTF.v().iv])
        memset("pool", ONES.v(), 1.0 / D)
        memset("pool", EPSB.v(), EPS)
        copy("dve", CTB.v(), CTF.v())

        def xvg(g, c, lo, hi):
            n = lo // TS
            assert hi <= (n + 1) * TS, (lo, hi)
            base = (xslot(g, n) * NCH + c) * TS
            return X.v(base + lo - n * TS, base + hi - n * TS)

        def load_tile(g, n):
            src = dr["xT"][g].rearrange("(c p) t -> p c t", p=128)[:, :, n * TS:(n + 1) * TS]
            dst = xtile3(g, n, 0, TS)
            dma("sp", f"x{g}_{n}", [(dst, src)], [], [xvg(g, c, n * TS, (n + 1) * TS).iv for c in range(NCH)])

        def load_tab(g):
            dma("sp", f"tab{g}", [(TAB[g].v().ap, dr["tab"][g])], [], [TAB[g].v().iv])

        def load_x(g):
            for n in range(NTILE):
                load_tile(g, n)
            load_tab(g)

        def store_y(g, c):
            dma("sp", f"st{g}_{c}", [(yT[g][128 * c:128 * (c + 1), :], xv(c, HALO, NT).ap)], [xv(c, HALO, NT).iv], [])

        def store_chunk_tile(g, n, c):
            lo = HALO if n == 0 else n * TS
            hi = (n + 1) * TS
            dma("sp", f"st{g}_{n}", [(yT[g][128 * c:128 * (c + 1), lo - HALO:hi - HALO], xv(c, lo, hi).ap)],
                [xv(c, lo, hi).iv], [])

        def store_tile(g, n):
            lo = HALO if n == 0 else n * TS
            hi = (n + 1) * TS
            src = xtile3(g, n, lo - n * TS, hi - n * TS)
            dst = yT[g].rearrange("(c p) t -> p c t", p=128)[:, :, lo - HALO:hi - HALO]
            dma("sp", f"st{g}_{n}", [(dst, src)], [xv(c, lo, hi).iv for c in range(NCH)], [])

        def ada_blocks(js):
            for j in js:
                blk = acquire(("ada", j))
                ps = psum(4)
                for mm in range(4):
                    o = V(ps.ap[:, mm:mm + 1], ps.iv)
                    mmgroup(o, [(blk.lhsT(k, 128 * mm), CTB.v(2 * k, 2 * k + 1)) for k in range(8)])
                tt("dve", MOD.v(4 * j, 4 * j + 4), ps, PRM.v(C_BADA + 4 * j, C_BADA + 4 * j + 4), ALU.add)
                release(blk)
            if 3 in js:
                stt("dve", DER.v(0, 8), MOD.v(8, 16), 1.0, PRM.v(C_GPRE1, C_GPRE1 + 8), ALU.add, ALU.mult)
            if 5 in js:
                tt("dve", DER.v(8, 16), MOD.v(16, 24), PRM.v(C_GPOST1, C_GPOST1 + 8), ALU.mult)
            if 9 in js:
                stt("dve", DER.v(16, 24), MOD.v(32, 40), 1.0, PRM.v(C_GPRE2, C_GPRE2 + 8), ALU.add, ALU.mult)
            if 11 in js:
                tt("dve", DER.v(24, 32), MOD.v(40, 48), PRM.v(C_GPOST2, C_GPOST2 + 8), ALU.mult)

        def prenorm_phase(acol, shcol, between=None, sq_all_act=False):
            RS = sbuf(11264, F32, NT)
            TT = [sbuf(15488 + 4224 * s, F32, NT) for s in range(3)]
            for c in range(NCH):
                if sq_all_act or c % 2 == 0:
                    act(H.v(c * NT, (c + 1) * NT), xv(c, 0, NT), AF.Square)
                else:
                    tt("dve", H.v(c * NT, (c + 1) * NT), xv(c, 0, NT), xv(c, 0, NT), ALU.mult)
            for n in range(NTILE):
                ps = psum()
                mmgroup(ps, [(ONES.v(), hv(c, n)) for c in range(NCH)])
                ts("dve", RS.v(n * TS, (n + 1) * TS), ps, EPS, None, ALU.add)
            act(RS.v(), RS.v(), AF.Ln)
            act(RS.v(), RS.v(), AF.Exp, scale=-0.5)
            if between is not None:
                between()
            for c in range(NCH):
                t_ = TT[c % 3]
                tt("dve", t_.v(), xv(c, 0, NT), RS.v(), ALU.mult)
                act(H.v(c * NT, (c + 1) * NT), t_.v(), AF.Identity, bias=MOD.v(shcol + c, shcol + c + 1),
                    scale=der(acol + c))

        def postnorm_group(OSBF, scol, hook=None):
            RS = sbuf(39424, F32, NT)
            for c in range(NCH):
                ov = OSBF.v(c * NT, (c + 1) * NT)
                if c % 2 == 0:
                    act(H.v(c * NT, (c + 1) * NT), ov, AF.Square)
                else:
                    tt("dve", H.v(c * NT, (c + 1) * NT), ov, ov, ALU.mult)
            for n in range(NTILE):
                ps = psum()
                mmgroup(ps, [(ONES.v(), hv(c, n)) for c in range(NCH)])
                ts("dve", RS.v(n * TS, (n + 1) * TS), ps, EPS, None, ALU.add)
            act(RS.v(), RS.v(), AF.Ln)
            act(RS.v(), RS.v(), AF.Exp, scale=-0.5)
            for c in range(NCH):
                ov = OSBF.v(c * NT, (c + 1) * NT)
                tt("dve", ov, ov, RS.v(), ALU.mult)
                xs = xv(c, 0, NT)
                stt("dve", xs, ov, der(scol + c), xs, ALU.mult, ALU.add)
                if hook is not None:
                    hook(c)

        def m1_phase(g):
            UW = 16 + NT
            U = [sbuf(4288 * s, F32, UW) for s in range(2)]
            A = [sbuf(8576 + 4288 * s, F32, UW) for s in range(2)]
            Bb = [sbuf(17152 + 4288 * s, F32, UW) for s in range(2)]
            PG = [[sbuf(25728 + 2112 * (2 * ps_ + kk), BF16, NT) for kk in range(2)] for ps_ in range(2)]
            SIG = [sbuf(34176 + 1408 * s, F32, TS) for s in range(2)]
            T48 = [sbuf(36992 + 192 * s, F32, SPC) for s in range(2)]
            for s in range(2):
                memset("pool", U[s].v(0, 16), 0.0)
                memset("pool", A[s].v(0, 16), 0.0)
                memset("pool", Bb[s].v(0, 16), 0.0)
            blk = {}

            def stA(c):
                hb, j = divmod(c, 4)
                if j == 0:
                    blk["P", hb] = acquire(("P", g, hb))
                pblk = blk["P", hb]
                s = c % 2
                pgp = c // 2
                w = 2 ** (pgp + 1)
                for n in range(NTILE):
                    ps = psum()
                    mmgroup(ps, [(pblk.lhsT(k, 128 * j), hv(k, n)) for k in range(8)])
                    copy("act", U[s].v(16 + n * TS, 16 + (n + 1) * TS), ps)
                if j == 3:
                    release(pblk)
                tt("dve", U[s].v(16, 16 + SPC), U[s].v(16, 16 + SPC), TAB[g].v(0, SPC), ALU.mult)
                src = U[s]
                for st in range(pgp + 1):
                    dst = A[s] if st % 2 == 0 else Bb[s]
                    sh = 2 ** st
                    tt("dve", dst.v(16, 16 + NT), src.v(16, 16 + NT),
                       src.v(16 - sh, 16 - sh + NT), ALU.add)
                    src = dst
                pg = PG[pgp % 2][c % 2]
                stt("dve", pg.v(SPC, NT), src.v(16 + SPC, 16 + NT), 1.0 / w, U[s].v(16 + SPC, 16 + NT),
                    ALU.mult, ALU.subtract)
                tt("dve", T48[s].v(), src.v(16, 16 + SPC), TAB[g].v((1 + pgp) * SPC, (2 + pgp) * SPC), ALU.mult)
                tt("dve", pg.v(0, SPC), T48[s].v(), U[s].v(16, 16 + SPC), ALU.subtract)

            def stB(pgp):
                hb = pgp // 2
                if pgp == 0:
                    blk["WP"] = acquire(("WP", g))
                if pgp % 2 == 0:
                    blk["ZA", hb] = acquire(("ZA", g, hb))
                wp = blk["WP"]
                zblk = blk["ZA", hb]
                for mm in range(2):
                    co = 2 * pgp + mm
                    jz = co - 4 * hb
                    for n in range(NTILE):
                        psz = psum()
                        mmgroup(psz, [(zblk.lhsT(k, 128 * jz), hv(k, n)) for k in range(8)])
                        sg = SIG[(co * NTILE + n) % 2]
                        act(sg.v(), psz, AF.Sigmoid)
                        psy = psum()
                        mmgroup(psy, [(wp.lhsT(2 * pgp + kk, 128 * mm),
                                       PG[pgp % 2][kk].v(n * TS, (n + 1) * TS)) for kk in range(2)])
                        stt("dve", MRG.v(co * NT + n * TS, co * NT + (n + 1) * TS), psy, prm(C_PSC + co),
                            sg.v(), ALU.mult, ALU.mult)
                if pgp % 2 == 1:
                    release(zblk)
                if pgp == 3:
                    release(wp)

            for p in range(4):
                stA(2 * p)
                stA(2 * p + 1)
                if p >= 1:
                    stB(p - 1)
            stB(3)

        def m2a_phase(g):
            per = 16912
            UXS = [sbuf(per * s, F32, NT) for s in range(2)]
            UBS = [sbuf(per * s + 4224, F32, NT) for s in range(2)]
            VV = [sbuf(per * s + 8448, F32, 2 + NT) for s in range(2)]
            ACC = [sbuf(per * s + 12688, F32, NT) for s in range(2)]
            for s in range(2):
                memset("pool", VV[s].v(0, 2), 0.0)
            blk = {}

            def stA(c):
                hb, j = divmod(c, 4)
                if j == 0:
                    blk[hb] = (acquire(("UX", g, hb)), acquire(("UC", g, hb)), acquire(("UB", g, hb)))
                bx, bc, bb = blk[hb]
                s = c % 2
                for n in range(NTILE):
                    ps = psum()
                    mmgroup(ps, [(bx.lhsT(k, 128 * j), hv(k, n)) for k in range(8)])
                    copy("act", UXS[s].v(n * TS, (n + 1) * TS), ps)
                for n in range(NTILE):
                    ps = psum()
                    mmgroup(ps, [(bc.lhsT(k, 128 * j), hv(k, n)) for k in range(8)])
                    tt("dve", VV[s].v(2 + n * TS, 2 + (n + 1) * TS), ps, UXS[s].v(n * TS, (n + 1) * TS), ALU.mult)
                for n in range(NTILE):
                    ps = psum()
                    mmgroup(ps, [(bb.lhsT(k, 128 * j), hv(k, n)) for k in range(8)])
                    copy("act", UBS[s].v(n * TS, (n + 1) * TS), ps)
                tt("dve", VV[s].v(2, 2 + SPC), VV[s].v(2, 2 + SPC), TAB[g].v(0, SPC), ALU.mult)
                if j == 3:
                    release(bx)
                    release(bc)
                    release(bb)

            def stB(c):
                s = c % 2
                act(ACC[s].v(), VV[s].v(2, 2 + NT), AF.Identity, bias=prm(C_CB + c), scale=prm(C_CW + 16 + c))
                stt("dve", ACC[s].v(), VV[s].v(1, 1 + NT), prm(C_CW + 8 + c), ACC[s].v(), ALU.mult, ALU.add)
                stt("dve", ACC[s].v(), VV[s].v(0, NT), prm(C_CW + c), ACC[s].v(), ALU.mult, ALU.add)
                tt("dve", Q.v(c * NT, (c + 1) * NT), ACC[s].v(), UBS[s].v(), ALU.mult)

            for c in range(NCH + 1):
                if c == 4 and g == 0:
                    ada_blocks(range(6, 10))
                if c < NCH:
                    stA(c)
                if c >= 1:
                    stB(c - 1)

        def m2b_phase(g):
            SIG = [sbuf(1408 * s, F32, TS) for s in range(2)]
            TMP = [sbuf(2816 + 1408 * s, F32, TS) for s in range(2)]
            blks = []
            for hb in range(2):
                bz = acquire(("ZB", g, hb))
                bo = acquire(("BO", g, hb))
                blks.append((bz, bo))
            for n in range(NTILE):
                for c in range(NCH):
                    hb, j = divmod(c, 4)
                    bz, bo = blks[hb]
                    s = (c * NTILE + n) % 2
                    psz = psum()
                    mmgroup(psz, [(bz.lhsT(k, 128 * j), hv(k, n)) for k in range(8)])
                    act(SIG[s].v(), psz, AF.Sigmoid)
                    psy = psum()
                    mmgroup(psy, [(bo.lhsT(k, 128 * j), Q.v(k * NT + n * TS, k * NT + (n + 1) * TS))
                                  for k in range(8)])
                    tt("dve", TMP[s].v(), psy, SIG[s].v(), ALU.mult)
                    mv = MRG.v(c * NT + n * TS, c * NT + (n + 1) * TS)
                    tt("dve", mv, TMP[s].v(), mv, ALU.add)
            for (bz, bo) in blks:
                release(bz)
                release(bo)

        def m3_phase(g):
            OSBF = sbuf(0, F32, NCH * NT)
            wo = [acquire(("WO", g, hb)) for hb in range(2)]
            for n in range(NTILE):
                for c in range(NCH):
                    ps = psum()
                    mmgroup(ps, [(wo[c // 4].lhsT(k, 128 * (c % 4)), MRG.v(k * NT + n * TS, k * NT + (n + 1) * TS))
                                 for k in range(8)])
                    copy("dve" if c % 2 == 0 else "act", OSBF.v(c * NT + n * TS, c * NT + (n + 1) * TS), ps)
            for b in wo:
                release(b)
            postnorm_group(OSBF, 8)

        def m3n_phase(g):
            OSB = [sbuf(SB - 11264, F32, NCH * TS), sbuf(25440, F32, NCH * TS)]
            gtail = 2 * HB

            def gbuf(off, ncols):
                assert gtail + off + ncols * 4 <= GB
                return Buf(tG, "G", gtail + off, F32, ncols)

            RS = [gbuf(1408 * s, TS) for s in range(2)]
            RS2 = [gbuf(2816 + 1408 * s, TS) for s in range(2)]
            TT = [gbuf(5632 + 1408 * s, TS) for s in range(4)]
            wo = [acquire(("WO", g, hb)) for hb in range(2)]
            for n in range(NTILE):
                s = n % 2
                for c in range(NCH):
                    ps = psum()
                    mmgroup(ps, [(wo[c // 4].lhsT(k, 128 * (c % 4)), MRG.v(k * NT + n * TS, k * NT + (n + 1) * TS))
                                 for k in range(8)])
                    copy("act", OSB[s].v(c * TS, (c + 1) * TS), ps)
                if n == NTILE - 1:
                    for b in wo:
                        release(b)
                norm_chain_tile(n, OSB[s], RS[s], RS2[s], TT, 8, 16, 24)

        def postnorm_tile(n, ovf, RSt, scol, hook=None):
            for c in range(NCH):
                ov = ovf(c)
                if c % SQ_DVE_POST != SQ_DVE_POST - 1:
                    act(hv(c, n), ov, AF.Square)
                else:
                    tt("dve", hv(c, n), ov, ov, ALU.mult)
            ps = psum()
            mmgroup(ps, [(ONES.v(), hv(c, n)) for c in range(NCH)])
            act(RSt.v(), ps, AF.Ln, bias=EPSB.v())
            act(RSt.v(), RSt.v(), AF.Exp, scale=-0.5)
            for c in range(NCH):
                ov = ovf(c)
                tt("dve", ov, ov, RSt.v(), ALU.mult)
                xs = xv(c, n * TS, (n + 1) * TS)
                stt("dve", xs, ov, der(scol + c), xs, ALU.mult, ALU.add)
                if hook is not None:
                    hook(c)

        def prenorm_tile(n, RS2t, TT, acol, shcol):
            for c in range(NCH):
                xs = xv(c, n * TS, (n + 1) * TS)
                if c % SQ_DVE_PRE != SQ_DVE_PRE - 1:
                    act(hv(c, n), xs, AF.Square)
                else:
                    tt("dve", hv(c, n), xs, xs, ALU.mult)
            ps = psum()
            mmgroup(ps, [(ONES.v(), hv(c, n)) for c in range(NCH)])
            act(RS2t.v(), ps, AF.Ln, bias=EPSB.v())
            act(RS2t.v(), RS2t.v(), AF.Exp, scale=-0.5)
            for c in range(NCH):
                t_ = TT[c % len(TT)]
                tt("dve", t_.v(), xv(c, n * TS, (n + 1) * TS), RS2t.v(), ALU.mult)
                if c % 4 == 3:
                    ts("dve", hv(c, n), t_.v(), der(acol + c), MOD.v(shcol + c, shcol + c + 1), ALU.mult, ALU.add)
                else:
                    act(hv(c, n), t_.v(), AF.Identity, bias=MOD.v(shcol + c, shcol + c + 1), scale=der(acol + c))

        def norm_chain_tile(n, OSBt, RSt, RS2t, TT, scol, acol, shcol):
            postnorm_tile(n, lambda c: OSBt.v(c * TS, (c + 1) * TS), RSt, scol)
            prenorm_tile(n, RS2t, TT, acol, shcol)

        def prenorm1_tiles(g):
            RS2 = [sbuf(36608 + 1408 * s, F32, TS) for s in range(2)]
            TT = [sbuf(39424 + 1408 * s, F32, TS) for s in range(4)]
            for n in range(NTILE):
                prenorm_tile(n, RS2[n % 2], TT, 0, 0)

        def f1_phase(g):
            NRS = 4
            if g + 1 < NG:
                load_tile(g + 1, 0)
                load_tab(g + 1)
            RG = [sbuf(8480 * s, F32, 2 + NT) for s in range(NRS)]
            RV = [sbuf(8480 * s + 4240, F32, 2 + NT) for s in range(NRS)]
            AG = [sbuf(8480 * NRS + 8448 * s, F32, NT) for s in range(2)]
            AV = [sbuf(8480 * NRS + 8448 * s + 4224, F32, NT) for s in range(2)]
            for s in range(NRS):
                memset("pool", RG[s].v(0, 2), 0.0)
                memset("pool", RV[s].v(0, 2), 0.0)
            blk = {}

            def stA(f):
                fb, j = divmod(f, 4)
                if j == 0:
                    blk[fb] = (acquire(("FG", g, fb)), acquire(("FV", g, fb)))
                bg, bv = blk[fb]
                s = f % NRS
                for (b_, R) in ((bg, RG[s]), (bv, RV[s])):
                    for n in range(NTILE):
                        ps = psum()
                        mmgroup(ps, [(b_.lhsT(k, 128 * j), hv(k, n)) for k in range(8)])
                        copy("act", R.v(2 + n * TS, 2 + (n + 1) * TS), ps)
                    tt("dve", R.v(2, 2 + SPC), R.v(2, 2 + SPC), TAB[g].v(0, SPC), ALU.mult)
                if j == bg.wd // 128 - 1:
                    release(bg)
                    release(bv)

            def stB1(f):
                s = f % NRS
                a = f % 2
                for (R, Ab, ch) in ((RG[s], AG[a], f), (RV[s], AV[a], NFF + f)):
                    act(Ab.v(), R.v(2, 2 + NT), AF.Identity, bias=prm(C_FB + ch), scale=prm(C_FW + 88 + ch))
                    stt("dve", Ab.v(), R.v(1, 1 + NT), prm(C_FW + 44 + ch), Ab.v(), ALU.mult, ALU.add)
                    stt("dve", Ab.v(), R.v(0, NT), prm(C_FW + ch), Ab.v(), ALU.mult, ALU.add)

            def stB2(f):
                a = f % 2
                act(AG[a].v(), AG[a].v(), AF.Gelu_apprx_tanh)
                tt("dve", GBUF.v(f * NT, (f + 1) * NT), AG[a].v(), AV[a].v(), ALU.mult)

            for i in range(NFF + 2):
                if i < NFF:
                    stA(i)
                if 0 <= i - 1 < NFF:
                    stB1(i - 1)
                if 0 <= i - 2 < NFF:
                    stB2(i - 2)

        def f2_phase(g):
            OSBF = sbuf(0, F32, NCH * NT)
            FS = 14

            def mm_part(ps, bd, n, f0, f1):
                def fn(e):
                    last = None
                    for f in range(f0, f1):
                        last = e.matmul(ps.ap, bd.lhsT(f, 0).ap, GBUF.v(f * NT + n * TS, f * NT + (n + 1) * TS).ap,
                                        start=(f == 0), stop=(f == NFF - 1))
                    return last
                reads = []
                for f in range(f0, f1):
                    reads.append(bd.lhsT(f, 0).iv)
                    reads.append(GBUF.v(f * NT + n * TS, f * NT + (n + 1) * TS).iv)
                P.add("pe", fn, reads, [ps.iv], dur=(f1 - f0) * (TS * 0.000417 + 0.012))

            def evac(c, n, ps):
                copy("act" if (c * NTILE + n) % 2 == 0 else "dve",
                     OSBF.v(c * NT + n * TS, c * NT + (n + 1) * TS), ps)

            bds = [acquire(("DN", g, 0)), acquire(("DN", g, 1))]
            pss = {}
            for c in range(2):
                for n in range(NTILE):
                    pss[c, n] = psum()
                    mm_part(pss[c, n], bds[c], n, 0, FS)
            for c in range(2):
                for n in range(NTILE):
                    mm_part(pss[c, n], bds[c], n, FS, NFF)
                    evac(c, n, pss[c, n])
                release(bds[c])
            if not TILE_IO:
                for c in range(2, NCH):
                    bd = acquire(("DN", g, c))
                    for n in range(NTILE):
                        ps = psum()
                        mm_part(ps, bd, n, 0, NFF)
                        evac(c, n, ps)
                    release(bd)
                postnorm_group(OSBF, 24, hook=lambda c: store_y(g, c))
                return
            for c in range(2, NCH - TAILC):
                bd = acquire(("DN", g, c))
                for n in range(NTILE):
                    ps = psum()
                    mm_part(ps, bd, n, 0, NFF)
                    evac(c, n, ps)
                release(bd)
            tail = {c: acquire(("DN", g, c)) for c in range(NCH - TAILC, NCH)}
            RS = [sbuf(33792 + 1408 * s_, F32, TS) for s_ in range(2)]
            for n in range(NTILE):
                for c in range(NCH - TAILC, NCH):
                    ps = psum()
                    mm_part(ps, tail[c], n, 0, NFF)
                    evac(c, n, ps)
                if n == NTILE - 1:
                    for c in range(NCH - TAILC, NCH):
                        release(tail[c])
                if n == NTILE - 1:
                    postnorm_tile(n, lambda c, n=n: OSBF.v(c * NT + n * TS, c * NT + (n + 1) * TS), RS[n % 2], 24,
                                  hook=lambda c, n=n: store_chunk_tile(g, n, c))
                else:
                    postnorm_tile(n, lambda c, n=n: OSBF.v(c * NT + n * TS, c * NT + (n + 1) * TS), RS[n % 2], 24)
                    store_tile(g, n)
                    if g + 1 < NG:
                        load_tile(g + 1, n + 1)

        load_x(0)
        for s_ in range(NSLOT):
            issue(s_)
        for g in range(NG):
            cur_g[0] = g
            if TILE_IO:
                if g == 0:
                    ada_blocks(range(0, 4))
                prenorm1_tiles(g)
            else:
                prenorm_phase(0, 0, between=(lambda: ada_blocks(range(0, 4))) if g == 0 else None)
            m1_phase(g)
            if g == 0:
                ada_blocks(range(4, 6))
            m2a_phase(g)
            m2b_phase(g)
            if g == 0:
                ada_blocks(range(10, 12))
            if TILE_CHAIN:
                m3n_phase(g)
            else:
                m3_phase(g)
                prenorm_phase(16, 24, sq_all_act=True)
            f1_phase(g)
            f2_phase(g)
        assert cursor[0] == len(plan), (cursor[0], len(plan))

        order = P.schedule() if USE_SCHED else None
        P.emit(nc, block, eng_sems, key_sems, final_keys, order)
    return nc


def _cols(v, n):
    return np.ascontiguousarray(np.asarray(v, np.float32).reshape(n, 128).T)


def _host_inputs(inputs):
    x = np.asarray(inputs["x"], np.float32)
    c = np.asarray(inputs["c"], np.float32)
    prm = np.zeros((128, NPRM), np.float32)
    prm[:, C_GPRE1:C_GPRE1 + 8] = _cols(inputs["g_pre_mix"][0], 8)
    prm[:, C_GPOST1:C_GPOST1 + 8] = _cols(inputs["g_post_mix"][0], 8)
    prm[:, C_GPRE2:C_GPRE2 + 8] = _cols(inputs["g_pre_ffn"][0], 8)
    prm[:, C_GPOST2:C_GPOST2 + 8] = _cols(inputs["g_post_ffn"][0], 8)
    prm[:, C_BADA:C_BADA + 48] = _cols(inputs["b_ada"][0], 48)
    prm[:, C_PSC:C_PSC + 8] = _cols(inputs["pool_scale"][0], 8)
    for t in range(3):
        prm[:, C_CW + 8 * t:C_CW + 8 * t + 8] = _cols(inputs["conv_w"][0][t], 8)
        prm[:, C_FW + 44 * t:C_FW + 44 * t + 44] = _cols(inputs["ffn_conv_w"][0][t], 44)
    prm[:, C_CB:C_CB + 8] = _cols(inputs["conv_b"][0], 8)
    prm[:, C_FB:C_FB + 44] = _cols(inputs["ffn_conv_b"][0], 44)
    shared = {
        "prm": prm,
        "w_ada": np.ascontiguousarray(np.asarray(inputs["w_ada"], np.float32)[0]),
        "w_in": np.ascontiguousarray(np.asarray(inputs["w_in"], np.float32)[0]),
        "w_pool": np.ascontiguousarray(np.asarray(inputs["w_pool"], np.float32)[0].reshape(D, 256)),
        "w_bout": np.ascontiguousarray(np.asarray(inputs["w_bout"], np.float32)[0]),
        "w_o": np.ascontiguousarray(np.asarray(inputs["w_o"], np.float32)[0]),
        "w_up": np.ascontiguousarray(np.asarray(inputs["w_up"], np.float32)[0]),
        "w_down": np.ascontiguousarray(np.asarray(inputs["w_down"], np.float32)[0]),
    }
    windows = (2, 4, 8, 16)
    in_maps = []
    for i in range(N_CORES):
        b = i // 4
        t0 = (i % 4) * CORE_TOK
        xT = np.zeros((NG, D, NT), np.float32)
        tab = np.ones((NG, 128, 5, SPC), np.float32)
        for g in range(NG):
            start = t0 + g * GT - HALO
            lo = max(start, 0)
            xT[g][:, lo - start:] = x[b, lo:start + NT, :].T
            for p in range(SPC):
                t = start + p
                if t < 0:
                    tab[g, :, 0, p] = 0.0
                else:
                    for k, w in enumerate(windows):
                        tab[g, :, 1 + k, p] = 1.0 / min(t + 1, w)
        m = dict(shared)
        m["xT"] = xT
        m["tab"] = tab.reshape(NG, 128, 5 * SPC)
        cT = np.zeros((128, 16), np.float32)
        cT[:, 0::2] = _cols(c[b], 8)
        m["cT"] = cT
        in_maps.append(m)
    return in_maps


def kernel(**inputs):
    in_maps = _host_inputs(inputs)
    nc = build_program()
    res = run_bass_kernel_spmd(nc, in_maps, core_ids=list(range(N_CORES)))
    out = np.zeros((2, 8192, D), np.float32)
    for i in range(N_CORES):
        b = i // 4
        t0 = (i % 4) * CORE_TOK
        y = res.results[i]["yT"]
        for g in range(NG):
            out[b, t0 + g * GT:t0 + (g + 1) * GT, :] = y[g].T
    return out
```

```python
import numpy as np
import concourse.bass as bass
import concourse.mybir as mybir
from concourse.bass_utils import run_bass_kernel_spmd

F32 = mybir.dt.float32
BF16 = mybir.dt.bfloat16
AF = mybir.ActivationFunctionType
ALU = mybir.AluOpType

D = 1024
DFF = 2816
NCH = 8
NFF = 22
NG = 2
GT = 1024
HALO = 20
NT = GT + HALO
TS = 348
NTILE = NT // TS
SPC = 48
EPS = 1e-6
NSLOT = 6
SLOT_ELEMS = 4096
N_CORES = 8
USE_SCHED = True
TILE_CHAIN = True
TILE_IO = True
TAILC = 4
XSLOTS = 4
SQ_DVE_POST = 4
SQ_DVE_PRE = 8
CORE_TOK = 2048

C_GPRE1, C_GPOST1, C_GPRE2, C_GPOST2 = 0, 8, 16, 24
C_BADA = 32
C_PSC = 80
C_CW = 88
C_CB = 112
C_FW = 120
C_FB = 252
NPRM = 296


class V:
    __slots__ = ("ap", "iv")

    def __init__(self, ap, iv):
        self.ap = ap
        self.iv = iv


class Buf:
    def __init__(self, t, arena, off_b, dtype, ncols):
        self.t, self.arena, self.off, self.dtype, self.ncols = t, arena, off_b, dtype, ncols
        self.es = 4 if dtype == F32 else 2
        assert off_b % 4 == 0

    def v(self, lo=0, hi=None):
        if hi is None:
            hi = self.ncols
        assert 0 <= lo < hi <= self.ncols, (lo, hi, self.ncols)
        b0 = self.off + lo * self.es
        b1 = self.off + hi * self.es
        assert b0 % 2 == 0
        a = self.t[:, b0 // 2: b1 // 2]
        if self.dtype == F32:
            assert b0 % 4 == 0
            a = a.bitcast(F32)
        return V(a, (self.arena, b0, b1))


class Prog:
    COMPUTE = ("pe", "act", "dve", "pool")

    def __init__(self):
        self.ops = []
        self.recs = {}

    def add(self, eng, fn, reads, writes, key=None, ndma=0, dur=0.5, xfer=0.0):
        oid = len(self.ops)
        deps = {}
        isdma = key is not None
        for (a, lo, hi) in reads:
            for r in self.recs.setdefault(a, []):
                if r[3] and r[0] < hi and lo < r[1]:
                    deps[r[2]] = True
        for (a, lo, hi) in writes:
            for r in self.recs.setdefault(a, []):
                if r[0] < hi and lo < r[1]:
                    deps.setdefault(r[2], False)
        for (a, lo, hi) in writes:
            L = self.recs[a]
            L[:] = [r for r in L if not (lo <= r[0] and r[1] <= hi)]
            L.append([lo, hi, oid, True, eng, isdma])
        for (a, lo, hi) in reads:
            L = self.recs[a]
            L.append([lo, hi, oid, False, eng, isdma])
        deps.pop(oid, None)
        self.ops.append(dict(eng=eng, fn=fn, deps=deps, key=key, ndma=ndma, dur=dur, xfer=xfer))
        return oid

    def kept_deps(self):
        ops = self.ops
        n = len(ops)
        kept = [None] * n
        for i, op in enumerate(ops):
            kd = []
            for d, raw in op["deps"].items():
                dop = ops[d]
                d_dma = dop["key"] is not None
                o_dma = op["key"] is not None
                if not d_dma and not o_dma and dop["eng"] == op["eng"]:
                    if op["eng"] == "pe":
                        continue
                kd.append(d)
            kept[i] = kd
        return kept

    def schedule(self):
        ops = self.ops
        n = len(ops)
        LAT = 0.7
        succ = [[] for _ in range(n)]
        indeg = [0] * n
        last_sp = None
        for i, op in enumerate(ops):
            ds = set(op["deps"].keys())
            if op["eng"] == "sp":
                if last_sp is not None:
                    ds.add(last_sp)
                last_sp = i
            for d in ds:
                succ[d].append(i)
                indeg[i] += 1
        blev = [0.0] * n
        for i in range(n - 1, -1, -1):
            m = 0.0
            for j in succ[i]:
                if blev[j] > m:
                    m = blev[j]
            blev[i] = m + ops[i]["dur"] + ops[i]["xfer"] + LAT
        t_eng = {e: 7.0 for e in ("pe", "act", "dve", "pool", "sp")}
        pipe = [0.0]
        DMA_FIXED = 2.0
        ready = {e: [] for e in t_eng}
        rtime = [0.0] * n
        done = [0.0] * n
        for i in range(n):
            if indeg[i] == 0:
                ready[ops[i]["eng"]].append(i)
        order = []
        while len(order) < n:
            best = None
            for e, L in ready.items():
                if not L:
                    continue
                te = t_eng[e]
                bi = None
                bkey = None
                for i in L:
                    st = rtime[i] if rtime[i] > te else te
                    key = (st, -blev[i], i)
                    if bkey is None or key < bkey:
                        bkey = key
                        bi = i
                if best is None or bkey < best[0]:
                    best = (bkey, e, bi)
            (st, _, _), e, i = best
            ready[e].remove(i)
            op = ops[i]
            t_eng[e] = st + op["dur"]
            if op["key"] is not None:
                beg = max(st + op["dur"], pipe[0])
                pipe[0] = beg + op["xfer"] - DMA_FIXED
                done[i] = max(beg + op["xfer"] - DMA_FIXED, st + op["dur"]) + DMA_FIXED
            else:
                done[i] = st + op["dur"]
            op["t0"] = st
            order.append(i)
            for j in succ[i]:
                same = (ops[j]["eng"] == e) and op["key"] is None and ops[j]["key"] is None
                if i not in ops[j]["deps"]:
                    r = st + op["dur"]
                else:
                    r = done[i] + (0.05 if same else LAT)
                if r > rtime[j]:
                    rtime[j] = r
                indeg[j] -= 1
                if indeg[j] == 0:
                    ready[ops[j]["eng"]].append(j)
        self.est_makespan = max(done) if n else 0.0
        return order

    def emit(self, nc, block, eng_sems, key_sems, final_keys, order=None):
        ops = self.ops
        n = len(ops)
        if order is None:
            order = list(range(n))
        kept = self.kept_deps()
        signals = [False] * n
        for i in range(n):
            for d in kept[i]:
                signals[d] = True
        cnt = {e: 0 for e in self.COMPUTE}
        cum = {}
        sigval = [0] * n
        semof = [None] * n
        for i in order:
            op = ops[i]
            if op["key"] is not None:
                k = op["key"]
                cum[k] = cum.get(k, 0) + 16 * op["ndma"]
                sigval[i] = cum[k]
                semof[i] = key_sems[k]
            else:
                if signals[i]:
                    cnt[op["eng"]] += 1
                sigval[i] = cnt[op["eng"]]
                semof[i] = eng_sems[op["eng"]]
        for e, c in cnt.items():
            assert c < 30000, (e, c)
        by_eng = {e: [] for e in ("pe", "act", "dve", "pool", "sp")}
        pos = {}
        for p_, i in enumerate(order):
            by_eng[ops[i]["eng"]].append(i)
            pos[i] = p_
        for i in range(n):
            for d in ops[i]["deps"]:
                assert pos[d] < pos[i], "order is not topological"

        def run(engname, e):
            waited = {}
            for i in by_eng[engname]:
                op = ops[i]
                need = {}
                for d in kept[i]:
                    s = semof[d]
                    sid = id(s)
                    if sigval[d] > need.get(sid, (None, 0))[1]:
                        need[sid] = (s, sigval[d])
                for sid, (s, val) in need.items():
                    if waited.get(sid, 0) >= val:
                        continue
                    e.wait_ge(s, val)
                    waited[sid] = val
                res = op["fn"](e)
                if op["key"] is not None:
                    assert len(res) == op["ndma"]
                    for ins in res:
                        ins.then_inc(semof[i], 16)
                elif signals[i]:
                    res.then_inc(semof[i], 1)
            if engname == "sp":
                for k in final_keys:
                    e.wait_ge(key_sems[k], cum[k])

        @block.tensor
        def _(e):
            run("pe", e)

        @block.scalar
        def _(e):
            run("act", e)

        @block.vector
        def _(e):
            run("dve", e)

        @block.gpsimd
        def _(e):
            run("pool", e)

        @block.sync
        def _(e):
            run("sp", e)


def weight_block_plan():
    plan = []

    def ada(js):
        for j in js:
            plan.append((("ada", j), "w_ada", 8, 512 * j, 512, "A"))

    for g in range(NG):
        if g == 0:
            ada(range(0, 4))
        plan.append((("P", g, 0), "w_in", 8, 0, 512, "A"))
        plan.append((("WP", g), "w_pool", 8, 0, 256, "A"))
        plan.append((("ZA", g, 0), "w_in", 8, 4096, 512, "A"))
        plan.append((("P", g, 1), "w_in", 8, 512, 512, "A"))
        plan.append((("ZA", g, 1), "w_in", 8, 4096 + 512, 512, "A"))
        if g == 0:
            ada(range(4, 6))
        for hb in range(2):
            if g == 0 and hb == 1:
                ada(range(6, 10))
            plan.append((("UX", g, hb), "w_in", 8, 1024 + 512 * hb, 512, "A"))
            plan.append((("UC", g, hb), "w_in", 8, 3072 + 512 * hb, 512, "A"))
            plan.append((("UB", g, hb), "w_in", 8, 2048 + 512 * hb, 512, "A"))
        for hb in range(2):
            plan.append((("ZB", g, hb), "w_in", 8, 5120 + 512 * hb, 512, "A"))
            plan.append((("BO", g, hb), "w_bout", 8, 512 * hb, 512, "A"))
        if g == 0:
            ada(range(10, 12))
        for hb in range(2):
            plan.append((("WO", g, hb), "w_o", 8, 512 * hb, 512, "A"))
        for fb in range(6):
            wd = 512 if fb < 5 else 256
            plan.append((("FG", g, fb), "w_up", 8, 512 * fb, wd, "A"))
            plan.append((("FV", g, fb), "w_up", 8, DFF + 512 * fb, wd, "A"))
        for c in range(NCH):
            plan.append((("DN", g, c), "w_down", NFF, 128 * c, 128, "A"))
    return plan


def build_program():
    nc = bass.Bass("TRN2", target_bir_lowering=False)
    dr = {}
    dr["xT"] = nc.dram_tensor("xT", [NG, D, NT], F32, kind="ExternalInput").ap()
    dr["tab"] = nc.dram_tensor("tab", [NG, 128, 5 * SPC], F32, kind="ExternalInput").ap()
    dr["prm"] = nc.dram_tensor("prm", [128, NPRM], F32, kind="ExternalInput").ap()
    dr["cT"] = nc.dram_tensor("cT", [128, 16], F32, kind="ExternalInput").ap()
    dr["w_ada"] = nc.dram_tensor("w_ada", [D, 6 * D], F32, kind="ExternalInput").ap()
    dr["w_in"] = nc.dram_tensor("w_in", [D, 6 * D], F32, kind="ExternalInput").ap()
    dr["w_pool"] = nc.dram_tensor("w_pool", [D, 256], F32, kind="ExternalInput").ap()
    dr["w_bout"] = nc.dram_tensor("w_bout", [D, D], F32, kind="ExternalInput").ap()
    dr["w_o"] = nc.dram_tensor("w_o", [D, D], F32, kind="ExternalInput").ap()
    dr["w_up"] = nc.dram_tensor("w_up", [D, 2 * DFF], F32, kind="ExternalInput").ap()
    dr["w_down"] = nc.dram_tensor("w_down", [DFF, D], F32, kind="ExternalInput").ap()
    yT = nc.dram_tensor("yT", [NG, D, GT], F32, kind="ExternalOutput").ap()

    XB = XSLOTS * NCH * TS * 4
    HB = NCH * NT * 2
    GB = NFF * NT * 2
    SB = 50816
    SMB = 4096

    from contextlib import ExitStack
    with ExitStack() as es:
        tX = es.enter_context(nc.sbuf_tensor("aX", [128, XB // 2], BF16))
        tH = es.enter_context(nc.sbuf_tensor("aH", [128, HB // 2], BF16))
        tG = es.enter_context(nc.sbuf_tensor("aG", [128, GB // 2], BF16))
        tS = es.enter_context(nc.sbuf_tensor("aS", [128, SB // 2], BF16))
        tM = es.enter_context(nc.sbuf_tensor("aM", [128, SMB // 2], BF16))
        tR = [es.enter_context(nc.sbuf_tensor(f"aR{s}", [128, SLOT_ELEMS], BF16)) for s in range(NSLOT)]
        tP = [es.enter_context(nc.psum_tensor(f"ps{b}", [128, 512], F32)) for b in range(8)]

        eng_sems = {e: es.enter_context(nc.semaphore(f"s_{e}")) for e in Prog.COMPUTE}
        key_names = [f"ring{s}" for s in range(NSLOT)] + ["small"] + [f"tab{g}" for g in range(NG)]
        nio = NTILE if TILE_IO else NCH
        key_names += [f"x{g}_{n}" for g in range(NG) for n in range(nio)]
        key_names += [f"st{g}_{n}" for g in range(NG) for n in range(nio)]
        key_sems = {k: es.enter_context(nc.semaphore(f"k_{k}")) for k in key_names}
        final_keys = [f"st{g}_{n}" for g in range(NG) for n in range(nio)]
        block = es.enter_context(nc.Block())

        P = Prog()

        X = Buf(tX, "X", 0, F32, XSLOTS * NCH * TS)
        cur_g = [0]
        H = Buf(tH, "H", 0, BF16, NCH * NT)
        MRG = Buf(tG, "G", 0, BF16, NCH * NT)
        Q = Buf(tG, "G", HB, BF16, NCH * NT)
        GBUF = Buf(tG, "G", 0, BF16, NFF * NT)

        def xslot(g, n):
            return (NTILE * g + n) % XSLOTS

        def xv(c, lo, hi):
            n = lo // TS
            assert hi <= (n + 1) * TS, (lo, hi)
            base = (xslot(cur_g[0], n) * NCH + c) * TS
            return X.v(base + lo - n * TS, base + hi - n * TS)

        def xtile3(g, n, lo, hi):
            sl = xslot(g, n)
            return X.v(sl * NCH * TS, (sl + 1) * NCH * TS).ap.rearrange("p (c t) -> p c t", c=NCH)[:, :, lo:hi]

        def hv(c, n):
            return H.v(c * NT + n * TS, c * NT + (n + 1) * TS)

        so = [0]

        def small(dtype, ncols):
            es_ = 4 if dtype == F32 else 2
            b = Buf(tM, "M", so[0], dtype, ncols)
            so[0] += (ncols * es_ + 3) // 4 * 4
            assert so[0] <= SMB
            return b

        PRM = small(F32, NPRM)
        MOD = small(F32, 48)
        DER = small(F32, 32)
        CTF = small(F32, 16)
        CTB = small(BF16, 16)
        ONES = small(BF16, 128)
        EPSB = small(F32, 1)
        TAB = [small(F32, 5 * SPC) for _ in range(NG)]

        def prm(col):
            return PRM.v(col, col + 1)

        def der(col):
            return DER.v(col, col + 1)

        def sbuf(off, dtype, ncols):
            es_ = 4 if dtype == F32 else 2
            assert off + ncols * es_ <= SB, (off, ncols)
            return Buf(tS, "S", off, dtype, ncols)

        pctr = [0]

        def psum(ncols=TS):
            b = pctr[0] % 8
            pctr[0] += 1
            return V(tP[b][:, 0:ncols], (f"ps{b}", 0, ncols * 4))

        def sc_ap(x):
            return x.ap if isinstance(x, V) else x

        def sc_iv(*xs):
            return [x.iv for x in xs if isinstance(x, V)]

        def ncols(v):
            return int(v.ap.shape[-1])

        def dve_dur(out):
            return (ncols(out) + 151) / 960.0

        def act(out, in_, func, bias=0.0, scale=1.0):
            d = 0.13 + ncols(out) * 0.00083 + (0.1 if in_.iv[0].startswith("ps") else 0.0) + \
                (0.12 if isinstance(scale, V) else 0.0)
            P.add("act", lambda e: e.activation(out.ap, in_.ap, func, bias=sc_ap(bias), scale=sc_ap(scale)),
                  [in_.iv] + sc_iv(bias, scale), [out.iv], dur=d)

        def tt(eng, out, a, b, op):
            P.add(eng, lambda e: e.tensor_tensor(out.ap, a.ap, b.ap, op), [a.iv, b.iv], [out.iv], dur=dve_dur(out))

        def ts(eng, out, a, s1, s2, op0, op1=None):
            if op1 is None:
                P.add(eng, lambda e: e.tensor_scalar(out.ap, a.ap, sc_ap(s1), None, op0),
                      [a.iv] + sc_iv(s1), [out.iv], dur=dve_dur(out))
            else:
                P.add(eng, lambda e: e.tensor_scalar(out.ap, a.ap, sc_ap(s1), sc_ap(s2), op0, op1),
                      [a.iv] + sc_iv(s1, s2), [out.iv], dur=dve_dur(out))

        def stt(eng, out, in0, scalar, in1, op0, op1):
            P.add(eng, lambda e: e.scalar_tensor_tensor(out.ap, in0.ap, sc_ap(scalar), in1.ap, op0, op1),
                  [in0.iv, in1.iv] + sc_iv(scalar), [out.iv], dur=dve_dur(out) + 0.06)

        def copy(eng, out, in_):
            if eng == "act":
                act(out, in_, AF.Copy)
            else:
                P.add(eng, lambda e: e.tensor_copy(out.ap, in_.ap), [in_.iv], [out.iv], dur=dve_dur(out))

        def memset(eng, out, val):
            P.add(eng, lambda e: e.memset(out.ap, val), [], [out.iv], dur=0.1)

        def mmgroup(out, pairs):
            def fn(e):
                last = None
                for i, (l, r) in enumerate(pairs):
                    last = e.matmul(out.ap, l.ap, r.ap, start=(i == 0), stop=(i == len(pairs) - 1))
                return last
            reads = []
            for l, r in pairs:
                reads.append(l.iv)
                reads.append(r.iv)
            P.add("pe", fn, reads, [out.iv], dur=len(pairs) * (ncols(out) * 0.000417 + 0.012))

        def dma(queue, key, pairs, reads, writes):
            def fn(e):
                return [e.dma_start(out=o, in_=i) for (o, i) in pairs]
            nbytes = 0
            for (o, i) in pairs:
                m = 4
                for d_ in i.shape:
                    m *= int(d_)
                nbytes += m
            P.add(queue, fn, reads, writes, key=key, ndma=len(pairs),
                  dur=(1.05 if queue == "pool" else 0.2) * len(pairs), xfer=2.0 + nbytes / 330e3)

        plan = weight_block_plan()
        issued = [False] * len(plan)
        cursor = [0]

        slot_of = {}
        free_slots = list(range(NSLOT))
        next_issue = [0]

        class Blk:
            def __init__(self, idx):
                self.idx = idx
                self.slot = slot_of[idx]
                tag, wname, nk, col0, wd, kind = plan[idx]
                self.nk, self.wd = nk, wd

            def lhsT(self, k, j0):
                lo = k * self.wd + j0
                return V(tR[self.slot][:, lo:lo + 128], (f"R{self.slot}", lo * 2, (lo + 128) * 2))

        def issue_more():
            while free_slots and next_issue[0] < len(plan):
                idx = next_issue[0]
                next_issue[0] += 1
                s = free_slots.pop(0)
                slot_of[idx] = s
                issued[idx] = True
                tag, wname, nk, col0, wd, kind = plan[idx]
                src = dr[wname][:, col0:col0 + wd].rearrange("(k p) j -> p k j", p=128)
                dst = tR[s][:, 0:nk * wd].rearrange("p (k j) -> p k j", k=nk)
                pairs = [(dst[:, k0:min(k0 + 8, nk), :], src[:, k0:min(k0 + 8, nk), :]) for k0 in range(0, nk, 8)]
                rd = []
                dma("pool", f"ring{s}", pairs, rd, [(f"R{s}", 0, nk * wd * 2)])

        def issue(idx):
            issue_more()

        def acquire(tag):
            idx = cursor[0]
            assert plan[idx][0] == tag, (plan[idx][0], tag)
            assert issued[idx], tag
            cursor[0] += 1
            return Blk(idx)

        def release(blk):
            free_slots.append(blk.slot)
            issue_more()

        dma("sp", "small", [(PRM.v().ap, dr["prm"][:, :]), (CTF.v().ap, dr["cT"][:, :])], [], [PRM.v().iv, CTF.v().iv])
        memset("pool", ONES.v(), 1.0 / D)
        memset("pool", EPSB.v(), EPS)
        copy("dve", CTB.v(), CTF.v())

        def xvg(g, c, lo, hi):
            n = lo // TS
            assert hi <= (n + 1) * TS, (lo, hi)
            base = (xslot(g, n) * NCH + c) * TS
            return X.v(base + lo - n * TS, base + hi - n * TS)

        def load_tile(g, n):
            src = dr["xT"][g].rearrange("(c p) t -> p c t", p=128)[:, :, n * TS:(n + 1) * TS]
            dst = xtile3(g, n, 0, TS)
            dma("sp", f"x{g}_{n}", [(dst, src)], [], [xvg(g, c, n * TS, (n + 1) * TS).iv for c in range(NCH)])

        def load_tab(g):
            dma("sp", f"tab{g}", [(TAB[g].v().ap, dr["tab"][g])], [], [TAB[g].v().iv])

        def load_x(g):
            for n in range(NTILE):
                load_tile(g, n)
            load_tab(g)

        def store_y(g, c):
            dma("sp", f"st{g}_{c}", [(yT[g][128 * c:128 * (c + 1), :], xv(c, HALO, NT).ap)], [xv(c, HALO, NT).iv], [])

        def store_chunk_tile(g, n, c):
            lo = HALO if n == 0 else n * TS
            hi = (n + 1) * TS
            dma("sp", f"st{g}_{n}", [(yT[g][128 * c:128 * (c + 1), lo - HALO:hi - HALO], xv(c, lo, hi).ap)],
                [xv(c, lo, hi).iv], [])

        def store_tile(g, n):
            lo = HALO if n == 0 else n * TS
            hi = (n + 1) * TS
            src = xtile3(g, n, lo - n * TS, hi - n * TS)
            dst = yT[g].rearrange("(c p) t -> p c t", p=128)[:, :, lo - HALO:hi - HALO]
            dma("sp", f"st{g}_{n}", [(dst, src)], [xv(c, lo, hi).iv for c in range(NCH)], [])

        def ada_blocks(js):
            for j in js:
                blk = acquire(("ada", j))
                ps = psum(4)
                for mm in range(4):
                    o = V(ps.ap[:, mm:mm + 1], ps.iv)
                    mmgroup(o, [(blk.lhsT(k, 128 * mm), CTB.v(2 * k, 2 * k + 1)) for k in range(8)])
                tt("dve", MOD.v(4 * j, 4 * j + 4), ps, PRM.v(C_BADA + 4 * j, C_BADA + 4 * j + 4), ALU.add)
                release(blk)
            if 3 in js:
                stt("dve", DER.v(0, 8), MOD.v(8, 16), 1.0, PRM.v(C_GPRE1, C_GPRE1 + 8), ALU.add, ALU.mult)
            if 5 in js:
                tt("dve", DER.v(8, 16), MOD.v(16, 24), PRM.v(C_GPOST1, C_GPOST1 + 8), ALU.mult)
            if 9 in js:
                stt("dve", DER.v(16, 24), MOD.v(32, 40), 1.0, PRM.v(C_GPRE2, C_GPRE2 + 8), ALU.add, ALU.mult)
            if 11 in js:
                tt("dve", DER.v(24, 32), MOD.v(40, 48), PRM.v(C_GPOST2, C_GPOST2 + 8), ALU.mult)

        def prenorm_phase(acol, shcol, between=None, sq_all_act=False):
            RS = sbuf(11264, F32, NT)
            TT = [sbuf(15488 + 4224 * s, F32, NT) for s in range(3)]
            for c in range(NCH):
                if sq_all_act or c % 2 == 0:
                    act(H.v(c * NT, (c + 1) * NT), xv(c, 0, NT), AF.Square)
                else:
                    tt("dve", H.v(c * NT, (c + 1) * NT), xv(c, 0, NT), xv(c, 0, NT), ALU.mult)
            for n in range(NTILE):
                ps = psum()
                mmgroup(ps, [(ONES.v(), hv(c, n)) for c in range(NCH)])
                ts("dve", RS.v(n * TS, (n + 1) * TS), ps, EPS, None, ALU.add)
            act(RS.v(), RS.v(), AF.Ln)
            act(RS.v(), RS.v(), AF.Exp, scale=-0.5)
            if between is not None:
                between()
            for c in range(NCH):
                t_ = TT[c % 3]
                tt("dve", t_.v(), xv(c, 0, NT), RS.v(), ALU.mult)
                act(H.v(c * NT, (c + 1) * NT), t_.v(), AF.Identity, bias=MOD.v(shcol + c, shcol + c + 1),
                    scale=der(acol + c))

        def postnorm_group(OSBF, scol, hook=None):
            RS = sbuf(39424, F32, NT)
            for c in range(NCH):
                ov = OSBF.v(c * NT, (c + 1) * NT)
                if c % 2 == 0:
                    act(H.v(c * NT, (c + 1) * NT), ov, AF.Square)
                else:
                    tt("dve", H.v(c * NT, (c + 1) * NT), ov, ov, ALU.mult)
            for n in range(NTILE):
                ps = psum()
                mmgroup(ps, [(ONES.v(), hv(c, n)) for c in range(NCH)])
                ts("dve", RS.v(n * TS, (n + 1) * TS), ps, EPS, None, ALU.add)
            act(RS.v(), RS.v(), AF.Ln)
            act(RS.v(), RS.v(), AF.Exp, scale=-0.5)
            for c in range(NCH):
                ov = OSBF.v(c * NT, (c + 1) * NT)
                tt("dve", ov, ov, RS.v(), ALU.mult)
                xs = xv(c, 0, NT)
                stt("dve", xs, ov, der(scol + c), xs, ALU.mult, ALU.add)
                if hook is not None:
                    hook(c)

        def m1_phase(g):
            UW = 16 + NT
            U = [sbuf(4288 * s, F32, UW) for s in range(2)]
            A = [sbuf(8576 + 4288 * s, F32, UW) for s in range(2)]
            Bb = [sbuf(17152 + 4288 * s, F32, UW) for s in range(2)]
            PG = [[sbuf(25728 + 2112 * (2 * ps_ + kk), BF16, NT) for kk in range(2)] for ps_ in range(2)]
            SIG = [sbuf(34176 + 1408 * s, F32, TS) for s in range(2)]
            T48 = [sbuf(36992 + 192 * s, F32, SPC) for s in range(2)]
            for s in range(2):
                memset("pool", U[s].v(0, 16), 0.0)
                memset("pool", A[s].v(0, 16), 0.0)
                memset("pool", Bb[s].v(0, 16), 0.0)
            blk = {}

            def stA(c):
                hb, j = divmod(c, 4)
                if j == 0:
                    blk["P", hb] = acquire(("P", g, hb))
                pblk = blk["P", hb]
                s = c % 2
                pgp = c // 2
                w = 2 ** (pgp + 1)
                for n in range(NTILE):
                    ps = psum()
                    mmgroup(ps, [(pblk.lhsT(k, 128 * j), hv(k, n)) for k in range(8)])
                    copy("act", U[s].v(16 + n * TS, 16 + (n + 1) * TS), ps)
                if j == 3:
                    release(pblk)
                tt("dve", U[s].v(16, 16 + SPC), U[s].v(16, 16 + SPC), TAB[g].v(0, SPC), ALU.mult)
                src = U[s]
                for st in range(pgp + 1):
                    dst = A[s] if st % 2 == 0 else Bb[s]
                    sh = 2 ** st
                    tt("dve", dst.v(16, 16 + NT), src.v(16, 16 + NT),
                       src.v(16 - sh, 16 - sh + NT), ALU.add)
                    src = dst
                pg = PG[pgp % 2][c % 2]
                stt("dve", pg.v(SPC, NT), src.v(16 + SPC, 16 + NT), 1.0 / w, U[s].v(16 + SPC, 16 + NT),
                    ALU.mult, ALU.subtract)
                tt("dve", T48[s].v(), src.v(16, 16 + SPC), TAB[g].v((1 + pgp) * SPC, (2 + pgp) * SPC), ALU.mult)
                tt("dve", pg.v(0, SPC), T48[s].v(), U[s].v(16, 16 + SPC), ALU.subtract)

            def stB(pgp):
                hb = pgp // 2
                if pgp == 0:
                    blk["WP"] = acquire(("WP", g))
                if pgp % 2 == 0:
                    blk["ZA", hb] = acquire(("ZA", g, hb))
                wp = blk["WP"]
                zblk = blk["ZA", hb]
                for mm in range(2):
                    co = 2 * pgp + mm
                    jz = co - 4 * hb
                    for n in range(NTILE):
                        psz = psum()
                        mmgroup(psz, [(zblk.lhsT(k, 128 * jz), hv(k, n)) for k in range(8)])
                        sg = SIG[(co * NTILE + n) % 2]
                        act(sg.v(), psz, AF.Sigmoid)
                        psy = psum()
                        mmgroup(psy, [(wp.lhsT(2 * pgp + kk, 128 * mm),
                                       PG[pgp % 2][kk].v(n * TS, (n + 1) * TS)) for kk in range(2)])
                        stt("dve", MRG.v(co * NT + n * TS, co * NT + (n + 1) * TS), psy, prm(C_PSC + co),
                            sg.v(), ALU.mult, ALU.mult)
                if pgp % 2 == 1:
                    release(zblk)
                if pgp == 3:
                    release(wp)

            for p in range(4):
                stA(2 * p)
                stA(2 * p + 1)
                if p >= 1:
                    stB(p - 1)
            stB(3)

        def m2a_phase(g):
            per = 16912
            UXS = [sbuf(per * s, F32, NT) for s in range(2)]
            UBS = [sbuf(per * s + 4224, F32, NT) for s in range(2)]
            VV = [sbuf(per * s + 8448, F32, 2 + NT) for s in range(2)]
            ACC = [sbuf(per * s + 12688, F32, NT) for s in range(2)]
            for s in range(2):
                memset("pool", VV[s].v(0, 2), 0.0)
            blk = {}

            def stA(c):
                hb, j = divmod(c, 4)
                if j == 0:
                    blk[hb] = (acquire(("UX", g, hb)), acquire(("UC", g, hb)), acquire(("UB", g, hb)))
                bx, bc, bb = blk[hb]
                s = c % 2
                for n in range(NTILE):
                    ps = psum()
                    mmgroup(ps, [(bx.lhsT(k, 128 * j), hv(k, n)) for k in range(8)])
                    copy("act", UXS[s].v(n * TS, (n + 1) * TS), ps)
                for n in range(NTILE):
                    ps = psum()
                    mmgroup(ps, [(bc.lhsT(k, 128 * j), hv(k, n)) for k in range(8)])
                    tt("dve", VV[s].v(2 + n * TS, 2 + (n + 1) * TS), ps, UXS[s].v(n * TS, (n + 1) * TS), ALU.mult)
                for n in range(NTILE):
                    ps = psum()
                    mmgroup(ps, [(bb.lhsT(k, 128 * j), hv(k, n)) for k in range(8)])
                    copy("act", UBS[s].v(n * TS, (n + 1) * TS), ps)
                tt("dve", VV[s].v(2, 2 + SPC), VV[s].v(2, 2 + SPC), TAB[g].v(0, SPC), ALU.mult)
                if j == 3:
                    release(bx)
                    release(bc)
                    release(bb)

            def stB(c):
                s = c % 2
                act(ACC[s].v(), VV[s].v(2, 2 + NT), AF.Identity, bias=prm(C_CB + c), scale=prm(C_CW + 16 + c))
                stt("dve", ACC[s].v(), VV[s].v(1, 1 + NT), prm(C_CW + 8 + c), ACC[s].v(), ALU.mult, ALU.add)
                stt("dve", ACC[s].v(), VV[s].v(0, NT), prm(C_CW + c), ACC[s].v(), ALU.mult, ALU.add)
                tt("dve", Q.v(c * NT, (c + 1) * NT), ACC[s].v(), UBS[s].v(), ALU.mult)

            for c in range(NCH + 1):
                if c == 4 and g == 0:
                    ada_blocks(range(6, 10))
                if c < NCH:
                    stA(c)
                if c >= 1:
                    stB(c - 1)

        def m2b_phase(g):
            SIG = [sbuf(1408 * s, F32, TS) for s in range(2)]
            TMP = [sbuf(2816 + 1408 * s, F32, TS) for s in range(2)]
            blks = []
            for hb in range(2):
                bz = acquire(("ZB", g, hb))
                bo = acquire(("BO", g, hb))
                blks.append((bz, bo))
            for n in range(NTILE):
                for c in range(NCH):
                    hb, j = divmod(c, 4)
                    bz, bo = blks[hb]
                    s = (c * NTILE + n) % 2
                    psz = psum()
                    mmgroup(psz, [(bz.lhsT(k, 128 * j), hv(k, n)) for k in range(8)])
                    act(SIG[s].v(), psz, AF.Sigmoid)
                    psy = psum()
                    mmgroup(psy, [(bo.lhsT(k, 128 * j), Q.v(k * NT + n * TS, k * NT + (n + 1) * TS))
                                  for k in range(8)])
                    tt("dve", TMP[s].v(), psy, SIG[s].v(), ALU.mult)
                    mv = MRG.v(c * NT + n * TS, c * NT + (n + 1) * TS)
                    tt("dve", mv, TMP[s].v(), mv, ALU.add)
                    if n == NTILE - 1 and j == 3:
                        release(bz)
                        release(bo)

        def m3_phase(g):
            OSBF = sbuf(0, F32, NCH * NT)
            wo = [acquire(("WO", g, hb)) for hb in range(2)]
            for n in range(NTILE):
                for c in range(NCH):
                    ps = psum()
                    mmgroup(ps, [(wo[c // 4].lhsT(k, 128 * (c % 4)), MRG.v(k * NT + n * TS, k * NT + (n + 1) * TS))
                                 for k in range(8)])
                    copy("dve" if c % 2 == 0 else "act", OSBF.v(c * NT + n * TS, c * NT + (n + 1) * TS), ps)
            for b in wo:
                release(b)
            postnorm_group(OSBF, 8)

        def m3n_phase(g):
            OSB = [sbuf(SB - 11264, F32, NCH * TS), sbuf(25440, F32, NCH * TS)]
            gtail = 2 * HB

            def gbuf(off, ncols):
                assert gtail + off + ncols * 4 <= GB
                return Buf(tG, "G", gtail + off, F32, ncols)

            RS = [gbuf(1408 * s, TS) for s in range(2)]
            RS2 = [gbuf(2816 + 1408 * s, TS) for s in range(2)]
            TT = [gbuf(5632 + 1408 * s, TS) for s in range(4)]
            wo = [acquire(("WO", g, hb)) for hb in range(2)]
            for n in range(NTILE):
                s = n % 2
                for c in range(NCH):
                    ps = psum()
                    mmgroup(ps, [(wo[c // 4].lhsT(k, 128 * (c % 4)), MRG.v(k * NT + n * TS, k * NT + (n + 1) * TS))
                                 for k in range(8)])
                    copy("act", OSB[s].v(c * TS, (c + 1) * TS), ps)
                    act(hv(c, n), ps, AF.Square)
                if n == NTILE - 1:
                    for b in wo:
                        release(b)
                norm_chain_tile(n, OSB[s], RS[s], RS2[s], TT, 8, 16, 24, sq_done=True)

        def postnorm_tile(n, ovf, RSt, scol, hook=None, act_sq_done=False):
            for c in range(NCH):
                ov = ovf(c)
                if act_sq_done:
                    continue
                if c % SQ_DVE_POST != SQ_DVE_POST - 1:
                    act(hv(c, n), ov, AF.Square)
                else:
                    tt("dve", hv(c, n), ov, ov, ALU.mult)
            ps = psum()
            mmgroup(ps, [(ONES.v(), hv(c, n)) for c in range(NCH)])
            act(RSt.v(), ps, AF.Ln, bias=EPSB.v())
            act(RSt.v(), RSt.v(), AF.Exp, scale=-0.5)
            for c in range(NCH):
                ov = ovf(c)
                tt("dve", ov, ov, RSt.v(), ALU.mult)
                xs = xv(c, n * TS, (n + 1) * TS)
                stt("dve", xs, ov, der(scol + c), xs, ALU.mult, ALU.add)
                if hook is not None:
                    hook(c)

        def prenorm_tile(n, RS2t, TT, acol, shcol, dve_share=True):
            for c in range(NCH):
                xs = xv(c, n * TS, (n + 1) * TS)
                if (not dve_share) or c % SQ_DVE_PRE != SQ_DVE_PRE - 1:
                    act(hv(c, n), xs, AF.Square)
                else:
                    tt("dve", hv(c, n), xs, xs, ALU.mult)
            ps = psum()
            mmgroup(ps, [(ONES.v(), hv(c, n)) for c in range(NCH)])
            act(RS2t.v(), ps, AF.Ln, bias=EPSB.v())
            act(RS2t.v(), RS2t.v(), AF.Exp, scale=-0.5)
            for c in range(NCH):
                t_ = TT[c % len(TT)]
                tt("dve", t_.v(), xv(c, n * TS, (n + 1) * TS), RS2t.v(), ALU.mult)
                if dve_share and c % 4 == 3:
                    ts("dve", hv(c, n), t_.v(), der(acol + c), MOD.v(shcol + c, shcol + c + 1), ALU.mult, ALU.add)
                else:
                    act(hv(c, n), t_.v(), AF.Identity, bias=MOD.v(shcol + c, shcol + c + 1), scale=der(acol + c))

        def norm_chain_tile(n, OSBt, RSt, RS2t, TT, scol, acol, shcol, sq_done=False):
            postnorm_tile(n, lambda c: OSBt.v(c * TS, (c + 1) * TS), RSt, scol, act_sq_done=sq_done)
            prenorm_tile(n, RS2t, TT, acol, shcol, dve_share=False)

        def prenorm1_tiles(g):
            RS2 = [sbuf(36608 + 1408 * s, F32, TS) for s in range(2)]
            TT = [sbuf(39424 + 1408 * s, F32, TS) for s in range(4)]
            for n in range(NTILE):
                prenorm_tile(n, RS2[n % 2], TT, 0, 0)

        def f1_phase(g):
            NRS = 4
            if g + 1 < NG:
                load_tile(g + 1, 0)
                load_tab(g + 1)
            RG = [sbuf(8480 * s, F32, 2 + NT) for s in range(NRS)]
            RV = [sbuf(8480 * s + 4240, F32, 2 + NT) for s in range(NRS)]
            AG = [sbuf(8480 * NRS + 8448 * s, F32, NT) for s in range(2)]
            AV = [sbuf(8480 * NRS + 8448 * s + 4224, F32, NT) for s in range(2)]
            for s in range(NRS):
                memset("pool", RG[s].v(0, 2), 0.0)
                memset("pool", RV[s].v(0, 2), 0.0)
            blk = {}

            def stA(f):
                fb, j = divmod(f, 4)
                if j == 0:
                    blk[fb] = (acquire(("FG", g, fb)), acquire(("FV", g, fb)))
                bg, bv = blk[fb]
                s = f % NRS
                for (b_, R) in ((bg, RG[s]), (bv, RV[s])):
                    for n in range(NTILE):
                        ps = psum()
                        mmgroup(ps, [(b_.lhsT(k, 128 * j), hv(k, n)) for k in range(8)])
                        copy("act", R.v(2 + n * TS, 2 + (n + 1) * TS), ps)
                    tt("dve", R.v(2, 2 + SPC), R.v(2, 2 + SPC), TAB[g].v(0, SPC), ALU.mult)
                if j == bg.wd // 128 - 1:
                    release(bg)
                    release(bv)

            def stB1(f):
                s = f % NRS
                a = f % 2
                for (R, Ab, ch) in ((RG[s], AG[a], f), (RV[s], AV[a], NFF + f)):
                    act(Ab.v(), R.v(2, 2 + NT), AF.Identity, bias=prm(C_FB + ch), scale=prm(C_FW + 88 + ch))
                    stt("dve", Ab.v(), R.v(1, 1 + NT), prm(C_FW + 44 + ch), Ab.v(), ALU.mult, ALU.add)
                    stt("dve", Ab.v(), R.v(0, NT), prm(C_FW + ch), Ab.v(), ALU.mult, ALU.add)

            def stB2(f):
                a = f % 2
                act(AG[a].v(), AG[a].v(), AF.Gelu_apprx_tanh)
                tt("dve", GBUF.v(f * NT, (f + 1) * NT), AG[a].v(), AV[a].v(), ALU.mult)

            for i in range(NFF + 2):
                if i < NFF:
                    stA(i)
                if 0 <= i - 1 < NFF:
                    stB1(i - 1)
                if 0 <= i - 2 < NFF:
                    stB2(i - 2)

        def f2_phase(g):
            OSBF = sbuf(0, F32, NCH * NT)
            FS = 14

            def mm_part(ps, bd, n, f0, f1):
                def fn(e):
                    last = None
                    for f in range(f0, f1):
                        last = e.matmul(ps.ap, bd.lhsT(f, 0).ap, GBUF.v(f * NT + n * TS, f * NT + (n + 1) * TS).ap,
                                        start=(f == 0), stop=(f == NFF - 1))
                    return last
                reads = []
                for f in range(f0, f1):
                    reads.append(bd.lhsT(f, 0).iv)
                    reads.append(GBUF.v(f * NT + n * TS, f * NT + (n + 1) * TS).iv)
                P.add("pe", fn, reads, [ps.iv], dur=(f1 - f0) * (TS * 0.000417 + 0.012))

            def evac(c, n, ps):
                copy("act" if (c * NTILE + n) % 2 == 0 else "dve",
                     OSBF.v(c * NT + n * TS, c * NT + (n + 1) * TS), ps)

            bds = [acquire(("DN", g, 0)), acquire(("DN", g, 1))]
            pss = {}
            for c in range(2):
                for n in range(NTILE):
                    pss[c, n] = psum()
                    mm_part(pss[c, n], bds[c], n, 0, FS)
            for c in range(2):
                for n in range(NTILE):
                    mm_part(pss[c, n], bds[c], n, FS, NFF)
                    evac(c, n, pss[c, n])
                release(bds[c])
            if not TILE_IO:
                for c in range(2, NCH):
                    bd = acquire(("DN", g, c))
                    for n in range(NTILE):
                        ps = psum()
                        mm_part(ps, bd, n, 0, NFF)
                        evac(c, n, ps)
                    release(bd)
                postnorm_group(OSBF, 24, hook=lambda c: store_y(g, c))
                return
            for c in range(2, NCH - TAILC):
                bd = acquire(("DN", g, c))
                for n in range(NTILE):
                    ps = psum()
                    mm_part(ps, bd, n, 0, NFF)
                    evac(c, n, ps)
                release(bd)
            tail = {c: acquire(("DN", g, c)) for c in range(NCH - TAILC, NCH)}
            RS = [sbuf(33792 + 1408 * s_, F32, TS) for s_ in range(2)]
            for n in range(NTILE):
                for c in range(NCH - TAILC, NCH):
                    ps = psum()
                    mm_part(ps, tail[c], n, 0, NFF)
                    evac(c, n, ps)
                if n == NTILE - 1:
                    for c in range(NCH - TAILC, NCH):
                        release(tail[c])
                if n == NTILE - 1:
                    postnorm_tile(n, lambda c, n=n: OSBF.v(c * NT + n * TS, c * NT + (n + 1) * TS), RS[n % 2], 24,
                                  hook=lambda c, n=n: store_chunk_tile(g, n, c))
                else:
                    postnorm_tile(n, lambda c, n=n: OSBF.v(c * NT + n * TS, c * NT + (n + 1) * TS), RS[n % 2], 24)
                    store_tile(g, n)
                    if g + 1 < NG:
                        load_tile(g + 1, n + 1)

        load_x(0)
        for s_ in range(NSLOT):
            issue(s_)
        for g in range(NG):
            cur_g[0] = g
            if TILE_IO:
                if g == 0:
                    ada_blocks(range(0, 4))
                prenorm1_tiles(g)
            else:
                prenorm_phase(0, 0, between=(lambda: ada_blocks(range(0, 4))) if g == 0 else None)
            m1_phase(g)
            if g == 0:
                ada_blocks(range(4, 6))
            m2a_phase(g)
            m2b_phase(g)
            if g == 0:
                ada_blocks(range(10, 12))
            if TILE_CHAIN:
                m3n_phase(g)
            else:
                m3_phase(g)
                prenorm_phase(16, 24, sq_all_act=True)
            f1_phase(g)
            f2_phase(g)
        assert cursor[0] == len(plan), (cursor[0], len(plan))

        order = P.schedule() if USE_SCHED else None
        P.emit(nc, block, eng_sems, key_sems, final_keys, order)
    return nc


def _cols(v, n):
    return np.ascontiguousarray(np.asarray(v, np.float32).reshape(n, 128).T)


def _host_inputs(inputs):
    x = np.asarray(inputs["x"], np.float32)
    c = np.asarray(inputs["c"], np.float32)
    prm = np.zeros((128, NPRM), np.float32)
    prm[:, C_GPRE1:C_GPRE1 + 8] = _cols(inputs["g_pre_mix"][0], 8)
    prm[:, C_GPOST1:C_GPOST1 + 8] = _cols(inputs["g_post_mix"][0], 8)
    prm[:, C_GPRE2:C_GPRE2 + 8] = _cols(inputs["g_pre_ffn"][0], 8)
    prm[:, C_GPOST2:C_GPOST2 + 8] = _cols(inputs["g_post_ffn"][0], 8)
    prm[:, C_BADA:C_BADA + 48] = _cols(inputs["b_ada"][0], 48)
    prm[:, C_PSC:C_PSC + 8] = _cols(inputs["pool_scale"][0], 8)
    for t in range(3):
        prm[:, C_CW + 8 * t:C_CW + 8 * t + 8] = _cols(inputs["conv_w"][0][t], 8)
        prm[:, C_FW + 44 * t:C_FW + 44 * t + 44] = _cols(inputs["ffn_conv_w"][0][t], 44)
    prm[:, C_CB:C_CB + 8] = _cols(inputs["conv_b"][0], 8)
    prm[:, C_FB:C_FB + 44] = _cols(inputs["ffn_conv_b"][0], 44)
    shared = {
        "prm": prm,
        "w_ada": np.ascontiguousarray(np.asarray(inputs["w_ada"], np.float32)[0]),
        "w_in": np.ascontiguousarray(np.asarray(inputs["w_in"], np.float32)[0]),
        "w_pool": np.ascontiguousarray(np.asarray(inputs["w_pool"], np.float32)[0].reshape(D, 256)),
        "w_bout": np.ascontiguousarray(np.asarray(inputs["w_bout"], np.float32)[0]),
        "w_o": np.ascontiguousarray(np.asarray(inputs["w_o"], np.float32)[0]),
        "w_up": np.ascontiguousarray(np.asarray(inputs["w_up"], np.float32)[0]),
        "w_down": np.ascontiguousarray(np.asarray(inputs["w_down"], np.float32)[0]),
    }
    windows = (2, 4, 8, 16)
    in_maps = []
    for i in range(N_CORES):
        b = i // 4
        t0 = (i % 4) * CORE_TOK
        xT = np.zeros((NG, D, NT), np.float32)
        tab = np.ones((NG, 128, 5, SPC), np.float32)
        for g in range(NG):
            start = t0 + g * GT - HALO
            lo = max(start, 0)
            xT[g][:, lo - start:] = x[b, lo:start + NT, :].T
            for p in range(SPC):
                t = start + p
                if t < 0:
                    tab[g, :, 0, p] = 0.0
                else:
                    for k, w in enumerate(windows):
                        tab[g, :, 1 + k, p] = 1.0 / min(t + 1, w)
        m = dict(shared)
        m["xT"] = xT
        m["tab"] = tab.reshape(NG, 128, 5 * SPC)
        cT = np.zeros((128, 16), np.float32)
        cT[:, 0::2] = _cols(c[b], 8)
        m["cT"] = cT
        in_maps.append(m)
    return in_maps


def kernel(**inputs):
    in_maps = _host_inputs(inputs)
    nc = build_program()
    res = run_bass_kernel_spmd(nc, in_maps, core_ids=list(range(N_CORES)))
    out = np.zeros((2, 8192, D), np.float32)
    for i in range(N_CORES):
        b = i // 4
        t0 = (i % 4) * CORE_TOK
        y = res.results[i]["yT"]
        for g in range(NG):
            out[b, t0 + g * GT:t0 + (g + 1) * GT, :] = y[g].T
    return out
```
